# Optimizing a Trainium2 kernel written in Bass

```python
import math
import jax
import jax.numpy as jnp
from jax import lax
import numpy as np

D_MODEL = 2048
BATCH = 4
SEQ = 4096
DEPTH = 2
DEC_BATCH = 2
DEC_SEQ = 4096
PAST_LEN = 128

N_MIXERS = 2
N_ATTN = (DEPTH + N_MIXERS - 1) // N_MIXERS
N_CONV = DEPTH // N_MIXERS
HEAD_DIM = 64
N_HEADS = D_MODEL // (2 * HEAD_DIM)
Q_BLOCK = 128
CONV_WIDTH = 31
CONV_PAD = (CONV_WIDTH - 1) // 2
D_FF = 4 * D_MODEL
RMS_EPS = 1e-6
SUBLN_EPS = 1e-5
LN_EPS = 1e-5

kernel_name = "hybrid_diffattn_conformer_encoder"


def rmsnorm(x, g, eps=RMS_EPS):
    xf = x.astype(jnp.float32)
    xf = xf * lax.rsqrt(jnp.mean(xf * xf, axis=-1, keepdims=True) + eps)
    return (xf * g.astype(jnp.float32)).astype(x.dtype)


def layernorm(x, g, b, eps=LN_EPS):
    xf = x.astype(jnp.float32)
    mu = jnp.mean(xf, axis=-1, keepdims=True)
    var = jnp.mean(jnp.square(xf - mu), axis=-1, keepdims=True)
    y = (xf - mu) * lax.rsqrt(var + eps) * g.astype(jnp.float32) + b.astype(jnp.float32)
    return y.astype(x.dtype)


def alibi_slopes(n):
    return jnp.exp2(-8.0 * jnp.arange(1, n + 1, dtype=jnp.float32) / n)


def lambda_init_for(layer_idx):
    return 0.8 - 0.6 * math.exp(-0.3 * layer_idx)


def diff_attention(h, w_qkv, lam_q1, lam_k1, lam_q2, lam_k2, subln_g, w_o, lambda_init):
    B, S, _ = h.shape
    n_blk = S // Q_BLOCK
    q, k, v = jnp.split(h @ w_qkv, 3, axis=-1)
    q = q.reshape(B, n_blk, Q_BLOCK, 2, N_HEADS, HEAD_DIM).transpose(1, 0, 2, 3, 4, 5)
    k = k.reshape(B, S, 2, N_HEADS, HEAD_DIM)
    v = v.reshape(B, S, N_HEADS, 2 * HEAD_DIM)
    lam = (jnp.exp(jnp.sum(lam_q1.astype(jnp.float32) * lam_k1.astype(jnp.float32)))
           - jnp.exp(jnp.sum(lam_q2.astype(jnp.float32) * lam_k2.astype(jnp.float32)))
           + lambda_init)
    slopes = alibi_slopes(N_HEADS)
    k_pos = jnp.arange(S, dtype=jnp.int32)
    scale = HEAD_DIM ** -0.5

    def block(args):
        q_blk, blk = args
        q_pos = blk * Q_BLOCK + jnp.arange(Q_BLOCK, dtype=jnp.int32)
        dist = jnp.abs(q_pos[:, None] - k_pos[None, :]).astype(jnp.float32)
        bias = -slopes[:, None, None] * dist[None]
        s = jnp.einsum("bqmhd,bkmhd->bmhqk", q_blk, k).astype(jnp.float32) * scale + bias
        p = jax.nn.softmax(s, axis=-1)
        a = p[:, 0] - lam * p[:, 1]
        return jnp.einsum("bhqk,bkhe->bqhe", a.astype(v.dtype), v)

    o = lax.map(block, (q, jnp.arange(n_blk, dtype=jnp.int32)))
    o = o.transpose(1, 0, 2, 3, 4).reshape(B, S, N_HEADS, 2 * HEAD_DIM)
    o = rmsnorm(o, subln_g, SUBLN_EPS) * (1.0 - lambda_init)
    return o.reshape(B, S, D_MODEL) @ w_o


def conformer_conv(h, w_pw1, b_pw1, w_dw, b_dw, ln_g, ln_b, w_pw2, b_pw2):
    u = h @ w_pw1 + b_pw1
    a, g = jnp.split(u, 2, axis=-1)
    u = a * jax.nn.sigmoid(g)
    u = lax.conv_general_dilated(
        u, w_dw[:, None, :], window_strides=(1,), padding=[(CONV_PAD, CONV_PAD)],
        dimension_numbers=('NWC', 'WIO', 'NWC'), feature_group_count=D_MODEL) + b_dw
    u = jax.nn.silu(layernorm(u, ln_g, ln_b))
    return u @ w_pw2 + b_pw2


def sq_relu_mlp(h, w_up, w_down):
    return jnp.square(jax.nn.relu(h @ w_up)) @ w_down


def encoder_trunk(x, attn_norm_g, w_qkv, lam_q1, lam_k1, lam_q2, lam_k2, subln_g, w_o,
                  conv_norm_g, conv_w_pw1, conv_b_pw1, conv_w_dw, conv_b_dw, conv_ln_g,
                  conv_ln_b, conv_w_pw2, conv_b_pw2, mlp_norm_g, w_up, w_down, final_norm_g):
    for i in range(DEPTH):
        j = i // N_MIXERS
        if i % N_MIXERS == 0:
            x = x + diff_attention(rmsnorm(x, attn_norm_g[j]), w_qkv[j], lam_q1[j], lam_k1[j],
                                   lam_q2[j], lam_k2[j], subln_g[j], w_o[j], lambda_init_for(i))
        else:
            x = x + conformer_conv(rmsnorm(x, conv_norm_g[j]), conv_w_pw1[j], conv_b_pw1[j],
                                   conv_w_dw[j], conv_b_dw[j], conv_ln_g[j], conv_ln_b[j],
                                   conv_w_pw2[j], conv_b_pw2[j])
        x = x + sq_relu_mlp(rmsnorm(x, mlp_norm_g[i]), w_up[i], w_down[i])
    return rmsnorm(x, final_norm_g)


def setup_inputs(seed: int = 0) -> dict:
    key = jax.random.key(seed)
    ks = jax.random.split(key, 24)
    f32 = jnp.float32
    D = D_MODEL

    def nrm(k, shape, scale):
        return jax.random.normal(k, shape, dtype=f32) * scale

    def gain(k, shape):
        return 1.0 + 0.01 * jax.random.normal(k, shape, dtype=f32)

    return {
        "x_prompt": nrm(ks[0], (BATCH, SEQ, D), 1.0),
        "x_sample": nrm(ks[1], (DEC_BATCH, DEC_SEQ, D), 1.0),
        "attn_norm_g": gain(ks[2], (N_ATTN, D)),
        "w_qkv": nrm(ks[3], (N_ATTN, D, 3 * D), D ** -0.5),
        "lam_q1": nrm(ks[4], (N_ATTN, HEAD_DIM), 0.1),
        "lam_k1": nrm(ks[5], (N_ATTN, HEAD_DIM), 0.1),
        "lam_q2": nrm(ks[6], (N_ATTN, HEAD_DIM), 0.1),
        "lam_k2": nrm(ks[7], (N_ATTN, HEAD_DIM), 0.1),
        "subln_g": gain(ks[8], (N_ATTN, 2 * HEAD_DIM)),
        "w_o": nrm(ks[9], (N_ATTN, D, D), D ** -0.5),
        "conv_norm_g": gain(ks[10], (N_CONV, D)),
        "conv_w_pw1": nrm(ks[11], (N_CONV, D, 2 * D), D ** -0.5),
        "conv_b_pw1": nrm(ks[12], (N_CONV, 2 * D), 0.01),
        "conv_w_dw": nrm(ks[13], (N_CONV, CONV_WIDTH, D), CONV_WIDTH ** -0.5),
        "conv_b_dw": nrm(ks[14], (N_CONV, D), 0.01),
        "conv_ln_g": gain(ks[15], (N_CONV, D)),
        "conv_ln_b": nrm(ks[16], (N_CONV, D), 0.01),
        "conv_w_pw2": nrm(ks[17], (N_CONV, D, D), D ** -0.5),
        "conv_b_pw2": nrm(ks[18], (N_CONV, D), 0.01),
        "mlp_norm_g": gain(ks[19], (DEPTH, D)),
        "w_up": nrm(ks[20], (DEPTH, D, D_FF), D ** -0.5),
        "w_down": nrm(ks[21], (DEPTH, D_FF, D), D_FF ** -0.5),
        "final_norm_g": gain(ks[22], (D,)),
    }


def reference(x_prompt, x_sample, attn_norm_g, w_qkv, lam_q1, lam_k1, lam_q2, lam_k2, subln_g,
              w_o, conv_norm_g, conv_w_pw1, conv_b_pw1, conv_w_dw, conv_b_dw, conv_ln_g,
              conv_ln_b, conv_w_pw2, conv_b_pw2, mlp_norm_g, w_up, w_down, final_norm_g):
    y_prompt = encoder_trunk(x_prompt, attn_norm_g, w_qkv, lam_q1, lam_k1, lam_q2, lam_k2,
                             subln_g, w_o, conv_norm_g, conv_w_pw1, conv_b_pw1, conv_w_dw,
                             conv_b_dw, conv_ln_g, conv_ln_b, conv_w_pw2, conv_b_pw2,
                             mlp_norm_g, w_up, w_down, final_norm_g)
    y_sample = encoder_trunk(x_sample, attn_norm_g, w_qkv, lam_q1, lam_k1, lam_q2, lam_k2,
                             subln_g, w_o, conv_norm_g, conv_w_pw1, conv_b_pw1, conv_w_dw,
                             conv_b_dw, conv_ln_g, conv_ln_b, conv_w_pw2, conv_b_pw2,
                             mlp_norm_g, w_up, w_down, final_norm_g)
    return (y_prompt, y_sample)
```

```python
import math
from contextlib import ExitStack

import numpy as np
import concourse.bass as bass
import concourse.mybir as mybir
from concourse.bass_utils import run_bass_kernel_spmd

F32 = mybir.dt.float32
BF16 = mybir.dt.bfloat16
AF = mybir.ActivationFunctionType
ALU = mybir.AluOpType
AX = mybir.AxisListType

D = 2048
H = 16
HD = 64
DFF = 8192
TB = 512
NCH = 16
CW = 31
CPAD = 15
RMS_EPS = 1e-6
SUBLN_EPS = 1e-5
LN_EPS = 1e-5
SCALE = HD ** -0.5
LAMBDA_INIT0 = 0.8 - 0.6 * math.exp(-0.3 * 0)
SKIP_THRESH = 60.0
CASTS_IN_A = True
CASTS_IN_B = True
NBUSY = 40


class T:
    __slots__ = ("name", "w", "r", "sem", "ndma", "_lastdma")

    def __init__(self, name):
        self.name = name
        self.w = None
        self.r = []
        self.sem = None
        self.ndma = 0
        self._lastdma = None


class Sched:
    ENGS = ("pe", "act", "dve", "pool", "sp")

    def __init__(self, nc):
        self.nc = nc
        self.ops = []

    def _add(self, eng, fn, reads, writes, is_dma=False, grp=None):
        oid = len(self.ops)
        deps = set()
        for t in reads:
            if t.w is not None:
                deps.add(t.w)
        for t in writes:
            if t.w is not None:
                deps.add(t.w)
            deps.update(t.r)
        if is_dma:
            if grp._lastdma is not None:
                deps.add(grp._lastdma)
            grp._lastdma = oid
        for t in reads:
            t.r.append(oid)
        for t in writes:
            t.w = oid
            t.r = []
        deps.discard(oid)
        self.ops.append([eng, fn, deps, is_dma, grp, False, 0])
        return oid

    def op(self, eng, fn, reads=(), writes=()):
        return self._add(eng, fn, reads, writes)

    def dma(self, eng, fn, grp, reads=(), writes=()):
        return self._add(eng, fn, reads, writes, True, grp)

    def emit(self, stack):
        nc = self.nc
        ops = self.ops
        for o in ops:
            eng, is_dma = o[0], o[3]
            for d in o[2]:
                p = ops[d]
                if p[3] or is_dma or p[0] != eng or eng != "pe":
                    p[5] = True
        cnt = {e: 0 for e in self.ENGS}
        grps = []
        for o in ops:
            if o[3]:
                g = o[4]
                if g.sem is None:
                    g.sem = "pending"
                    grps.append(g)
                g.ndma += 1
                o[6] = 16 * g.ndma
                o[5] = True
            elif o[5]:
                cnt[o[0]] += 1
                o[6] = cnt[o[0]]
        esem = {e: stack.enter_context(nc.semaphore("s_" + e)) for e in self.ENGS}
        for i, g in enumerate(grps):
            g.sem = stack.enter_context(nc.semaphore("g%d" % i))
        self.nsem = len(grps) + 5
        per_eng = {e: [] for e in self.ENGS}
        for i, o in enumerate(ops):
            per_eng[o[0]].append(i)
        block = stack.enter_context(nc.Block())

        def run_engine(e, handle):
            waited = {}
            for i in per_eng[e]:
                eng, fn, deps, is_dma, grp, sig, count = ops[i]
                need = {}
                for d in deps:
                    p = ops[d]
                    if p[3]:
                        s = p[4].sem
                    elif is_dma or p[0] != eng or eng != "pe":
                        s = esem[p[0]]
                    else:
                        continue
                    k = id(s)
                    if need.get(k, (None, 0))[1] < p[6]:
                        need[k] = (s, p[6])
                for k, (s, v) in need.items():
                    if waited.get(k, 0) < v:
                        handle.wait_ge(s, v)
                        waited[k] = v
                ins = fn(handle)
                if sig:
                    if is_dma:
                        ins.then_inc(grp.sem, 16)
                    else:
                        ins.then_inc(esem[eng], 1)

        @block.tensor
        def _(h):
            run_engine("pe", h)

        @block.scalar
        def _(h):
            run_engine("act", h)

        @block.vector
        def _(h):
            run_engine("dve", h)

        @block.gpsimd
        def _(h):
            run_engine("pool", h)

        @block.sync
        def _(h):
            run_engine("sp", h)


class Phase:
    def __init__(self, nc, name):
        self.nc = nc
        self.name = name
        self.stack = ExitStack()
        self.S = Sched(nc)
        self.tiles = []
        self._rr = {}

    def T(self, name):
        t = T(name)
        self.tiles.append(t)
        return t

    def sb(self, name, shape, dt, nT=1):
        ap = self.stack.enter_context(self.nc.sbuf_tensor(self.name + "_" + name, shape, dt))
        if nT == 1:
            return ap, self.T(name)
        return ap, [self.T(name + str(i)) for i in range(nT)]

    def ps(self, name):
        ap = self.stack.enter_context(self.nc.psum_tensor(self.name + "_" + name, [128, 512], F32))
        return ap, self.T(name)

    def ring(self, key, n):
        i = self._rr.get(key, 0)
        self._rr[key] = i + 1
        return i % n

    def finish(self):
        S = self.S
        tb = T("bar")
        S.op("sp", lambda e: e.nop(), reads=(), writes=self.tiles + [tb])
        for eng in ("pe", "act", "dve", "pool"):
            S.op(eng, lambda e: e.nop(), reads=[tb])
        S.emit(self.stack)
        self.stack.close()


class WStream:
    def __init__(self, P, plan, nslots=3, shape=(128, NCH, 512), name="w"):
        self.P = P
        self.plan = plan
        self.n = nslots
        self.slots = []
        for i in range(nslots):
            ap, t = P.sb("%sr%d" % (name, i), list(shape), BF16)
            self.slots.append((ap, t))
        self.issued = 0
        self.used = 0

    def _issue(self):
        i = self.issued
        ap, t = self.slots[i % self.n]
        src = self.plan[i]
        self.P.S.dma("sp", lambda e, ap=ap, src=src: e.dma_start(out=ap[:], in_=src), t, writes=[t])
        self.issued += 1

    def next(self):
        while self.issued < len(self.plan) and self.issued < self.used + self.n:
            self._issue()
        ap, t = self.slots[self.used % self.n]
        self.used += 1
        return ap, t

    def prefetch(self):
        while self.issued < len(self.plan) and self.issued < self.used + self.n:
            self._issue()


def alibi_slope(h):
    return 2.0 ** (-8.0 * (h + 1) / H)


def build(S, dbg=False):
    NB = S // TB
    NKC = S // 128
    nc = bass.Bass("TRN2", target_bir_lowering=False)

    def din(name, shape):
        return nc.dram_tensor(name, list(shape), F32, kind="ExternalInput").ap()

    x = din("x", [S, D])
    attn_norm_g = din("attn_norm_g", [D])
    w_qkv = din("w_qkv", [D, 3 * D])
    lam_in = [din(n, [1, HD]) for n in ("lam_q1", "lam_k1", "lam_q2", "lam_k2")]
    subln_g = din("subln_g", [2 * HD])
    w_o = din("w_o", [D, D])
    conv_norm_g = din("conv_norm_g", [D])
    w_pw1 = din("conv_w_pw1", [D, 2 * D])
    b_pw1 = din("conv_b_pw1", [2 * D])
    w_dw = din("conv_w_dw", [CW, D])
    b_dw = din("conv_b_dw", [D])
    ln_g = din("conv_ln_g", [D])
    ln_b = din("conv_ln_b", [D])
    w_pw2 = din("conv_w_pw2", [D, D])
    b_pw2 = din("conv_b_pw2", [D])
    mlp_norm_g = din("mlp_norm_g", [2, D])
    w_up = din("w_up", [2, D, DFF])
    w_down = din("w_down", [2, DFF, D])
    final_norm_g = din("final_norm_g", [D])
    y = nc.dram_tensor("y", [S, D], F32, kind="ExternalOutput").ap()

    def scratch(name, shape, dt):
        return nc.dram_tensor(name, list(shape), dt, **skind).ap()

    skind = dict(kind="ExternalOutput") if dbg else {}
    wb_qkv = nc.dram_tensor("wb_qkv", [12, 128, NCH, 512], BF16).ap()
    wb_o = nc.dram_tensor("wb_o", [4, 128, NCH, 512], BF16).ap()
    wb_pw1 = nc.dram_tensor("wb_pw1", [8, 128, NCH, 512], BF16).ap()
    wb_pw2 = nc.dram_tensor("wb_pw2", [4, 128, NCH, 512], BF16, **skind).ap()
    wb_up = nc.dram_tensor("wb_up", [2, 16, 128, NCH, 512], BF16, **skind).ap()
    wb_down = nc.dram_tensor("wb_down", [2, 16, 128, NCH, 512], BF16, **skind).ap()
    DIAG = nc.dram_tensor("diag", [NCH, 128, CW, 128], BF16).ap()
    QT = scratch("QT", [H, 128, S], BF16)
    KT = scratch("KT", [H, 128, S], BF16)
    Vs = scratch("Vs", [S, D], BF16)
    OT = scratch("OT", [D, S], BF16)
    X2T = scratch("X2T", [D, S], F32)
    GLU = scratch("GLU", [D, S], BF16)

    outer = ExitStack()
    with outer:
        def psb(name, shape, dt):
            return outer.enter_context(nc.sbuf_tensor(name, shape, dt))

        NCST = 448
        cst = psb("cst", [128, NCST], F32)
        wdw = psb("wdwc", [128, NCH, 32], F32)
        identF = psb("identF", [128, 128], F32)
        identB = psb("identB", [128, 128], BF16)
        onesB = psb("onesB", [128, 128], BF16)
        CG_A, CG_C, CG_M0, CG_M1, CG_F, CB_PW1, CB_DW, CLN_G, CLN_B, CB_PW2, CSUB = [32 * i for i in range(11)]
        C_NLAM = 352 + 1
        C_GSUB = 352 + 2

        def col(base, c):
            return cst[:, base + c:base + c + 1]

        P = Phase(nc, "p0")
        S_ = P.S
        stage, _ = P.sb("stage", [32, 11 * 128], F32)
        stage2, t_stage2 = P.sb("stage2", [CW, D], F32)
        lamv, t_lamv = P.sb("lamv", [128, 4, HD], F32)
        ltmp, t_ltmp = P.sb("ltmp", [128, 2 * HD + 8], F32)
        dg, t_dg = P.sb("dg", [128, 2, CW, 128], BF16, nT=2)
        ps0, t_ps0 = P.ps("ps0")
        ps1, t_ps1 = P.ps("ps1")
        t_cst = P.T("cst")
        t_wdw = P.T("wdw")
        t_idF = P.T("idF")
        t_idB = P.T("idB")
        t_ones = P.T("ones")

        NG = 4
        cgrp = [P.T("cg%d" % i) for i in range(NG)]
        t_wbdram = T("wbdram")

        def cast(dst, src):
            g = cgrp[P.ring("cg", NG)]
            S_.dma("pool", lambda e, dst=dst, src=src: e.dma_start(out=dst, in_=src), g, writes=[g])

        def cast_std(dst_tiles, src2d, kgroups, ncolblk, col0=0):
            for g in range(kgroups):
                for j in range(ncolblk):
                    src = src2d[g * 2048:(g + 1) * 2048, col0 + j * 512:col0 + (j + 1) * 512]
                    cast(dst_tiles[g * ncolblk + j], src.rearrange("(c p) m -> p c m", p=128))

        cast_std([wb_qkv[j] for j in range(12)], w_qkv, 1, 12)

        S_.op("pool", lambda e: e.memset(identF[:], 0.0), writes=[t_idF])
        S_.op("pool", lambda e: e.affine_select(out=identF[:], in_=identF[:], pattern=[[-1, 128]], compare_op=ALU.not_equal,
                                                fill=1.0, base=0, channel_multiplier=1), reads=[t_idF], writes=[t_idF])
        S_.op("pool", lambda e: e.tensor_copy(out=identB[:], in_=identF[:]), reads=[t_idF], writes=[t_idB])
        S_.op("pool", lambda e: e.memset(onesB[:], 1.0), writes=[t_ones])

        vecs = [(attn_norm_g, 16), (conv_norm_g, 16), (mlp_norm_g[0], 16), (mlp_norm_g[1], 16), (final_norm_g, 16),
                (b_pw1, 32), (b_dw, 16), (ln_g, 16), (ln_b, 16), (b_pw2, 16), (subln_g, 1)]
        t_stage = []
        for i, (v, n) in enumerate(vecs):
            ts = P.T("stg%d" % i)
            t_stage.append(ts)
            S_.dma("sp", lambda e, i=i, v=v, n=n: e.dma_start(out=stage[0:n, i * 128:(i + 1) * 128],
                                                           in_=v.rearrange("(c p) -> c p", p=128)), ts, writes=[ts])

        def tr_vecs(e):
            for i, (v, n) in enumerate(vecs):
                ins = e.transpose(ps0[:, i * 32:i * 32 + n], stage[0:n, i * 128:(i + 1) * 128], identF[0:n, 0:n])
            return ins
        S_.op("pe", tr_vecs, reads=t_stage + [t_idF], writes=[t_ps0])
        S_.op("dve", lambda e: e.tensor_copy(out=cst[:, 0:352], in_=ps0[:, 0:352]), reads=[t_ps0], writes=[t_cst])
        S_.dma("sp", lambda e: e.dma_start(out=stage2[:], in_=w_dw), t_stage2, writes=[t_stage2])

        def tr_wdw(e):
            for c in range(NCH):
                ins = e.transpose(ps1[:, c * 32:c * 32 + CW], stage2[0:CW, c * 128:(c + 1) * 128], identF[0:CW, 0:CW])
            return ins
        S_.op("pe", tr_wdw, reads=[t_stage2, t_idF], writes=[t_ps1])
        S_.op("dve", lambda e: e.tensor_copy(out=wdw[:, :, 0:CW], in_=ps1[:].rearrange("p (c k) -> p c k", k=32)[:, :, 0:CW]),
              reads=[t_ps1], writes=[t_wdw])
        for i in range(4):
            S_.dma("sp", lambda e, i=i: e.dma_start(out=lamv[:, i, :], in_=lam_in[i].partition_broadcast(128)),
                   t_lamv, writes=[t_lamv])

        S_.op("dve", lambda e: e.tensor_tensor(out=ltmp[:, 0:HD], in0=lamv[:, 0, :], in1=lamv[:, 1, :], op=ALU.mult),
              reads=[t_lamv], writes=[t_ltmp])
        S_.op("dve", lambda e: e.tensor_tensor(out=ltmp[:, HD:2 * HD], in0=lamv[:, 2, :], in1=lamv[:, 3, :], op=ALU.mult),
              reads=[t_lamv], writes=[t_ltmp])
        S_.op("dve", lambda e: e.tensor_reduce(out=ltmp[:, 2 * HD:2 * HD + 2], in_=ltmp[:, 0:2 * HD].rearrange("p (a b) -> p a b", a=2),
                                               axis=AX.X, op=ALU.add), reads=[t_ltmp], writes=[t_ltmp])
        S_.op("act", lambda e: e.activation(out=ltmp[:, 2 * HD + 2:2 * HD + 4], in_=ltmp[:, 2 * HD:2 * HD + 2], func=AF.Exp),
              reads=[t_ltmp], writes=[t_ltmp])
        S_.op("dve", lambda e: e.tensor_tensor(out=ltmp[:, 2 * HD + 4:2 * HD + 5], in0=ltmp[:, 2 * HD + 2:2 * HD + 3],
                                               in1=ltmp[:, 2 * HD + 3:2 * HD + 4], op=ALU.subtract), reads=[t_ltmp], writes=[t_ltmp])
        S_.op("dve", lambda e: e.tensor_scalar(out=cst[:, C_NLAM:C_NLAM + 1], in0=ltmp[:, 2 * HD + 4:2 * HD + 5], scalar1=LAMBDA_INIT0,
                                               scalar2=-1.0, op0=ALU.add, op1=ALU.mult), reads=[t_ltmp, t_cst], writes=[t_cst])
        S_.op("dve", lambda e: e.tensor_scalar(out=cst[:, C_GSUB:C_GSUB + 1], in0=cst[:, CSUB:CSUB + 1],
                                               scalar1=(1.0 - LAMBDA_INIT0), scalar2=None, op0=ALU.mult), reads=[t_cst], writes=[t_cst])

        def emit_casts(P, jobs, ngroups=4, reads=()):
            if not hasattr(P, "cgrp"):
                P.cgrp = [P.T("cg%d" % i) for i in range(ngroups)]
            for dst, src in jobs:
                g = P.cgrp[P.ring("cg", ngroups)]
                P.S.dma("pool", lambda e, dst=dst, src=src: e.dma_start(out=dst, in_=src), g, reads=list(reads), writes=[g])

        def cast_jobs(dst_tiles, src2d, kgroups, ncolblk):
            jobs = []
            for g in range(kgroups):
                for j in range(ncolblk):
                    src = src2d[g * 2048:(g + 1) * 2048, j * 512:(j + 1) * 512]
                    jobs.append((dst_tiles[g * ncolblk + j], src.rearrange("(c p) m -> p c m", p=128)))
            return jobs

        late_jobs = cast_jobs([wb_o[j] for j in range(4)], w_o, 1, 4)
        late_jobs += cast_jobs([wb_up[0, j] for j in range(16)], w_up[0], 1, 16)
        late_jobs += cast_jobs([wb_down[0, j] for j in range(16)], w_down[0], 4, 4)
        late_jobs += cast_jobs([wb_pw1[j] for j in range(8)], w_pw1, 1, 8)
        late_jobs2 = cast_jobs([wb_pw2[j] for j in range(4)], w_pw2, 1, 4)
        late_jobs2 += cast_jobs([wb_up[1, j] for j in range(16)], w_up[1], 1, 16)
        late_jobs2 += cast_jobs([wb_down[1, j] for j in range(16)], w_down[1], 4, 4)
        if not CASTS_IN_B:
            late_jobs += late_jobs2
            late_jobs2 = []

        if not CASTS_IN_A:
            emit_casts(P, late_jobs)
        P.finish()

        def rstd_from(P, src_ap, dst_ap, t_src, t_dst, mult, eps):
            S_ = P.S
            S_.op("dve", lambda e: e.tensor_scalar(out=dst_ap, in0=src_ap, scalar1=mult, scalar2=eps,
                                                   op0=ALU.mult, op1=ALU.add), reads=[t_src], writes=[t_dst])

            S_.op("act", lambda e: e.activation(out=dst_ap, in_=dst_ap, func=AF.Ln), reads=[t_dst], writes=[t_dst])
            S_.op("act", lambda e: e.activation(out=dst_ap, in_=dst_ap, func=AF.Exp, scale=-0.5), reads=[t_dst], writes=[t_dst])

        def dense(P, ws, act, t_act, ntiles, banks, evac, swap=False):
            S_ = P.S
            for ti in range(ntiles):
                w, t_w = ws.next()
                for mc in range(4):
                    bi = P.ring("bank", len(banks))
                    bank, t_bank = banks[bi]

                    def mm(e, w=w, mc=mc, bank=bank):
                        for c in range(NCH):
                            if swap:
                                ins = e.matmul(bank[:], lhsT=act[:, c, mc * 128:(mc + 1) * 128], rhs=w[:, c, :],
                                               start=(c == 0), stop=(c == NCH - 1))
                            else:
                                ins = e.matmul(bank[:], lhsT=w[:, c, mc * 128:(mc + 1) * 128], rhs=act[:, c, :],
                                               start=(c == 0), stop=(c == NCH - 1))
                        return ins
                    if ti == 0 and mc == 0:
                        for c in range(NCH):
                            def mm1(e, w=w, bank=bank, c=c):
                                if swap:
                                    return e.matmul(bank[:], lhsT=act[:, c, 0:128], rhs=w[:, c, :], start=(c == 0), stop=(c == NCH - 1))
                                return e.matmul(bank[:], lhsT=w[:, c, 0:128], rhs=act[:, c, :], start=(c == 0), stop=(c == NCH - 1))
                            S_.op("pe", mm1, reads=[t_w, t_act[c]], writes=[t_bank])
                    else:
                        S_.op("pe", mm, reads=[t_w] + list(t_act), writes=[t_bank])
                    evac(ti, mc, bank, t_bank)
                    for _ in range(getattr(P, "bgn", 0)):
                        if P.bgq:
                            P.bgq.pop(0)()

        def norm_fm(P, x1T, t_x1, gbase, outT, t_out, psn, t_psn, rs, t_rs, sqb, t_sqb):
            S_ = P.S
            for c in range(NCH):
                r = P.ring("sqb", len(t_sqb))
                eng = "pool" if c % 2 == 0 else "act"
                if eng == "pool":
                    S_.op("pool", lambda e, c=c, r=r: e.tensor_tensor(out=sqb[:, r, :], in0=x1T[:, c, :], in1=x1T[:, c, :],
                                                                     op=ALU.mult), reads=[t_x1[c]], writes=[t_sqb[r]])
                else:
                    S_.op("act", lambda e, c=c, r=r: e.activation(out=sqb[:, r, :], in_=x1T[:, c, :], func=AF.Square),
                          reads=[t_x1[c]], writes=[t_sqb[r]])
                S_.op("pe", lambda e, c=c, r=r: e.matmul(psn[:], lhsT=onesB[:], rhs=sqb[:, r, :], start=(c == 0),
                                                        stop=(c == NCH - 1)), reads=[t_sqb[r]], writes=[t_psn])
            rstd_from(P, psn[:], rs[:], t_psn, t_rs, 1.0 / D, RMS_EPS)
            for c in range(NCH):
                S_.op("dve", lambda e, c=c: e.scalar_tensor_tensor(out=outT[:, c, :], in0=x1T[:, c, :], scalar=col(gbase, c),
                                                                  in1=rs[:], op0=ALU.mult, op1=ALU.mult),
                      reads=[t_x1[c], t_rs], writes=[t_out[c]])

        def mlp(P, ws, x1T, t_x1, hT, t_hT, aT, t_aT, banks, rtmp, t_rtmp):
            S_ = P.S
            for g in range(4):
                def evac_up(ti, mc, bank, t_bank):
                    fc = ti * 4 + mc
                    r = P.ring("rtmp", len(t_rtmp))
                    S_.op("act", lambda e, r=r, bank=bank: e.activation(out=rtmp[:, r, :], in_=bank[:], func=AF.Relu),
                          reads=[t_bank], writes=[t_rtmp[r]])
                    S_.op("pool", lambda e, r=r, fc=fc: e.tensor_tensor(out=aT[:, fc, :], in0=rtmp[:, r, :], in1=rtmp[:, r, :],
                                                                       op=ALU.mult), reads=[t_rtmp[r]], writes=[t_aT[fc]])
                dense(P, ws, hT, t_hT, 4, banks, evac_up)

                def evac_dn(ti, mc, bank, t_bank):
                    dc = ti * 4 + mc
                    S_.op("dve", lambda e, dc=dc, bank=bank: e.tensor_tensor(out=x1T[:, dc, :], in0=x1T[:, dc, :], in1=bank[:],
                                                                            op=ALU.add), reads=[t_bank, t_x1[dc]], writes=[t_x1[dc]])
                dense(P, ws, aT, t_aT, 4, banks, evac_dn)

        def mlp_plan(l):
            plan = []
            for g in range(4):
                plan += [wb_up[l, 4 * g + jj] for jj in range(4)]
                plan += [wb_down[l, g * 4 + j] for j in range(4)]
            return plan

        P = Phase(nc, "pA")
        S_ = P.S
        pace, _ = P.sb("pace", [128, 16], F32)
        cast_chunks = []
        if CASTS_IN_A:
            per = (len(late_jobs) + NB - 1) // NB
            cast_chunks = [late_jobs[i * per:(i + 1) * per] for i in range(NB)]
        xt2, t_xt2 = P.sb("xt", [128, 2, 4, D], F32, nT=8)
        junk, t_junk = P.sb("junk", [128, D], BF16)
        ssq2, t_ssq2 = P.sb("ssq", [128, 2, 8], F32, nT=2)
        hT2, t_hT2 = P.sb("hT", [128, 2, NCH, TB], BF16, nT=2 * NCH)
        sqk, t_sqk = P.sb("sqk", [128, 2, 4, TB], BF16, nT=2)
        sv, t_sv = P.sb("sv", [128, 2, 4, 512], BF16, nT=2)
        pst = [P.ps("pst%d" % i) for i in range(2)]
        banks = [P.ps("pso%d" % i) for i in range(4)]
        t_scr = T("scrA")
        ws = WStream(P, [wb_qkv[j] for j in range(12)] * NB)

        def prep_front(tb):
            pb_ = tb % 2
            xt = xt2[:, pb_]
            t_xt = t_xt2[pb_ * 4:pb_ * 4 + 4]
            ssq = ssq2[:, pb_]
            t_ssq = t_ssq2[pb_]
            for tc in range(4):
                r0 = tb * TB + tc * 128
                S_.dma("sp", lambda e, tc=tc, r0=r0: e.dma_start(out=xt[:, tc, :], in_=x[r0:r0 + 128, :]), t_xt[tc],
                       writes=[t_xt[tc]])
            for tc in range(4):
                S_.op("act", lambda e, tc=tc: e.activation(out=junk[:], in_=xt[:, tc, :], func=AF.Square,
                                                          accum_out=ssq[:, tc:tc + 1]),
                      reads=[t_xt[tc]], writes=[t_junk, t_ssq])
            rstd_from(P, ssq[:, 0:4], ssq[:, 4:8], t_ssq, t_ssq, 1.0 / D, RMS_EPS)
            for tc in range(4):
                S_.op("dve", lambda e, tc=tc: e.tensor_scalar(out=xt[:, tc, :], in0=xt[:, tc, :], scalar1=ssq[:, 4 + tc:5 + tc],
                                                             scalar2=None, op0=ALU.mult), reads=[t_ssq, t_xt[tc]], writes=[t_xt[tc]])

        def prep_back(tb):
            pb_ = tb % 2
            xt = xt2[:, pb_]
            t_xt = t_xt2[pb_ * 4:pb_ * 4 + 4]
            hT = hT2[:, pb_]
            t_hT = t_hT2[pb_ * NCH:(pb_ + 1) * NCH]
            for c in range(NCH):
                pb, t_pb = pst[c % 2]

                def trf(e, c=c, pb=pb):
                    for tc in range(4):
                        ins = e.transpose(pb[:, tc * 128:(tc + 1) * 128], xt[:, tc, c * 128:(c + 1) * 128], identF[:])
                    return ins
                S_.op("pe", trf, reads=t_xt, writes=[t_pb])
                if c % 2 == 0:
                    S_.op("act", lambda e, c=c, pb=pb: e.activation(out=hT[:, c, :], in_=pb[:], func=AF.Identity, scale=col(CG_A, c)),
                          reads=[t_pb], writes=[t_hT[c]])
                else:
                    S_.op("dve", lambda e, c=c, pb=pb: e.tensor_scalar(out=hT[:, c, :], in0=pb[:], scalar1=col(CG_A, c), scalar2=None,
                                                                      op0=ALU.mult), reads=[t_pb], writes=[t_hT[c]])

        prep_front(0)
        ws.prefetch()
        prep_back(0)
        for tb in range(NB):
            hT = hT2[:, tb % 2]
            t_hT = t_hT2[(tb % 2) * NCH:(tb % 2 + 1) * NCH]
            if cast_chunks and cast_chunks[tb]:
                tp = P.T("pace%d" % tb)
                S_.op("dve", lambda e, tb=tb: e.memset(pace[:, tb % 16:tb % 16 + 1], 0.0), writes=[tp])
                emit_casts(P, cast_chunks[tb], reads=[tp])
            if tb + 1 < NB:
                prep_front(tb + 1)

            def evac_qk(ti, mc, bank, t_bank, tb=tb):
                b = ti % 2
                if mc % 2 == 0:
                    S_.op("act", lambda e: e.activation(out=sqk[:, b, mc, :], in_=bank[:], func=AF.Copy), reads=[t_bank],
                          writes=[t_sqk[b]])
                else:
                    S_.op("dve", lambda e: e.tensor_copy(out=sqk[:, b, mc, :], in_=bank[:]), reads=[t_bank], writes=[t_sqk[b]])
                if mc == 3:
                    jt = ti % 4
                    dst = (QT if ti < 4 else KT)[jt * 4:jt * 4 + 4, :, tb * TB:(tb + 1) * TB].rearrange("c p t -> p c t")
                    S_.dma("sp", lambda e: e.dma_start(out=dst, in_=sqk[:, b]), t_sqk[b], reads=[t_sqk[b]], writes=[t_scr])
            dense(P, ws, hT, t_hT, 8, banks, evac_qk)
            if tb + 1 < NB:
                prep_back(tb + 1)

            def evac_v(ti, mc, bank, t_bank, tb=tb):
                b = ti % 2
                if mc % 2 == 0:
                    S_.op("act", lambda e: e.activation(out=sv[:, b, mc, :], in_=bank[:], func=AF.Copy), reads=[t_bank],
                          writes=[t_sv[b]])
                else:
                    S_.op("dve", lambda e: e.tensor_copy(out=sv[:, b, mc, :], in_=bank[:]), reads=[t_bank], writes=[t_sv[b]])
                if mc == 3:
                    dst = Vs[tb * TB:(tb + 1) * TB, ti * 512:(ti + 1) * 512].rearrange("(tc p) e -> p tc e", p=128)
                    S_.dma("sp", lambda e: e.dma_start(out=dst, in_=sv[:, b]), t_sv[b], reads=[t_sv[b]], writes=[t_scr])
            dense(P, ws, hT, t_hT, 4, banks, evac_v, swap=True)
        P.finish()

        P = Phase(nc, "pB")
        S_ = P.S
        NU = 2 * S - 128
        OFF = S - 128
        AB, t_AB = P.sb("AB", [128, NU], F32)
        Dh, t_Dh = P.sb("Dh", [128, 2, NU], BF16, nT=2)
        Qh, _ = P.sb("Qh", [128, 2, S], BF16)
        Kh, _ = P.sb("Kh", [128, 2, S], BF16)
        t_Qh = [[P.T("Qh%d%d" % (b, m)) for m in range(2)] for b in range(2)]
        t_Kh = [[P.T("Kh%d%d" % (b, m)) for m in range(2)] for b in range(2)]
        Vh, t_Vh = P.sb("Vh", [128, 2, NKC, 128], BF16, nT=2)
        NE = 8
        eb, t_eb = P.sb("eb", [128, NE, 512], F32, nT=NE)
        NET = 10
        ET, t_ET = P.sb("ET", [128, NET, 512], BF16, nT=NET)
        rz, t_rz = P.sb("rz", [128, 2, 512], F32, nT=2)
        ot, t_ot = P.sb("ot", [128, 2, 512], F32, nT=2)
        oo, t_oo = P.sb("oo", [128, 2, 512], F32, nT=2)
        osq, t_osq = P.sb("osq", [128, 2, 512], BF16, nT=2)
        orr, t_orr = P.sb("orr", [128, 512], F32)
        on, t_on = P.sb("on", [128, 2, 512], BF16, nT=2)
        psS = [P.ps("psS%d" % i) for i in range(3)]
        psO = [P.ps("psO%d" % i) for i in range(2)]
        psZ = [P.ps("psZ%d" % i) for i in range(2)]
        psE, t_psE = P.ps("psE")
        t_scr = T("scrB")
        LAG = 3

        S_.op("pool", lambda e: e.iota(AB[:], pattern=[[1, NU]], base=-OFF, channel_multiplier=-1,
                                       allow_small_or_imprecise_dtypes=True), writes=[t_AB])
        S_.op("act", lambda e: e.activation(out=AB[:], in_=AB[:], func=AF.Abs), reads=[t_AB], writes=[t_AB])
        pool_busy = bool(late_jobs2)
        if late_jobs2:
            cgB = [P.T("cgB%d" % i) for i in range(4)]
            for dst, src in late_jobs2:
                g = cgB[P.ring("cgB", 4)]
                S_.dma("pool", lambda e, dst=dst, src=src: e.dma_start(out=dst, in_=src), g, writes=[g])
            late_jobs2 = []

        def needed(h, kc, qb):
            q0, k0 = qb * TB, kc * 128
            mind = max(k0 - (q0 + TB - 1), q0 - (k0 + 127), 0)
            return alibi_slope(h) * mind <= SKIP_THRESH

        deferred = []
        def epilogue_part2(h, qb, pb_):
            ob = P.ring("on", 2)
            S_.op("pe", lambda e: e.matmul(psE[:], lhsT=onesB[:], rhs=osq[:, pb_, :], start=True, stop=True), reads=[t_osq[pb_]],
                  writes=[t_psE])
            rstd_from(P, psE[:], orr[:], t_psE, t_orr, 1.0 / 128.0, SUBLN_EPS)
            S_.op("dve", lambda e: e.scalar_tensor_tensor(out=on[:, ob, :], in0=oo[:, pb_, :], scalar=cst[:, C_GSUB:C_GSUB + 1],
                                                          in1=orr[:], op0=ALU.mult, op1=ALU.mult),
                  reads=[t_oo[pb_], t_orr], writes=[t_on[ob]])
            S_.dma("sp", lambda e: e.dma_start(out=OT[h * 128:(h + 1) * 128, qb * TB:(qb + 1) * TB], in_=on[:, ob, :]), t_on[ob],
                   reads=[t_on[ob]], writes=[t_scr])

        def head_loads(h):
            hb = h % 2
            hp = h // 2
            pbuf = hp % 2
            slope = alibi_slope(h)
            off = (h % 2) * HD
            for m in range(2):
                S_.dma("sp", lambda e, m=m: e.dma_start(out=Qh[m * HD:(m + 1) * HD, hb, :], in_=QT[m * 8 + hp][off:off + HD, :]),
                       t_Qh[hb][m], writes=[t_Qh[hb][m]])
                S_.dma("sp", lambda e, m=m: e.dma_start(out=Kh[m * HD:(m + 1) * HD, hb, :], in_=KT[m * 8 + hp][off:off + HD, :]),
                       t_Kh[hb][m], writes=[t_Kh[hb][m]])
            S_.dma("sp", lambda e: e.dma_start(
                out=Vh[:, hb], in_=Vs[:, h * 128:(h + 1) * 128].rearrange("(kc p) e -> p kc e", p=128)), t_Vh[hb], writes=[t_Vh[hb]])
            S_.op("act", lambda e: e.activation(out=Dh[:, hb, :], in_=AB[:], func=AF.Exp, scale=-slope),
                  reads=[t_AB], writes=[t_Dh[hb]])

        steps = []
        blkno = 0
        for h in range(H):
            for qb in range(NB):
                kcs = [kc for kc in range(NKC) if needed(h, kc, qb)]
                for i, kc in enumerate(kcs):
                    steps.append((h, qb, kc, i == 0, i == len(kcs) - 1, blkno))
                blkno += 1

        def emit_av(p):
            slots, h, kc, first, last = p
            hb = h % 2

            def av(e):
                for m in range(2):
                    e.matmul(psO[m][0][:], lhsT=Vh[:, hb, kc, :], rhs=ET[:, slots[m], :], start=first, stop=last)
                    ins = e.matmul(psZ[m][0][:], lhsT=onesB[:], rhs=ET[:, slots[m], :], start=first, stop=last)
                return ins
            S_.op("pe", av, reads=[t_Vh[hb], t_ET[slots[0]], t_ET[slots[1]]],
                  writes=[psO[0][1], psO[1][1], psZ[0][1], psZ[1][1]])

        def epilogue_part1(h, qb, blk):
            pb_ = P.ring("epi", 2)
            for m in range(2):
                S_.op("act", lambda e, m=m: e.activation(out=rz[:, m, :], in_=psZ[m][0][:], func=AF.Ln),
                      reads=[psZ[m][1]], writes=[t_rz[m]])
                if m == 0:
                    S_.op("dve", lambda e: e.tensor_copy(out=ot[:, 0, :], in_=psO[0][0][:]), reads=[psO[0][1]], writes=[t_ot[0]])
                else:
                    S_.op("dve", lambda e: e.tensor_scalar(out=ot[:, 1, :], in0=psO[1][0][:], scalar1=cst[:, C_NLAM:C_NLAM + 1],
                                                           scalar2=None, op0=ALU.mult), reads=[psO[1][1]], writes=[t_ot[1]])
            for m in range(2):
                S_.op("act", lambda e, m=m: e.activation(out=rz[:, m, :], in_=rz[:, m, :], func=AF.Exp, scale=-1.0),
                      reads=[t_rz[m]], writes=[t_rz[m]])
            S_.op("dve", lambda e: e.tensor_tensor(out=ot[:, 0, :], in0=ot[:, 0, :], in1=rz[:, 0, :], op=ALU.mult),
                  reads=[t_ot[0], t_rz[0]], writes=[t_ot[0]])
            pe_ = "dve" if (pool_busy and blk < NBUSY) else "pool"
            S_.op(pe_, lambda e: e.tensor_tensor(out=ot[:, 1, :], in0=ot[:, 1, :], in1=rz[:, 1, :], op=ALU.mult),
                  reads=[t_ot[1], t_rz[1]], writes=[t_ot[1]])
            S_.op("dve", lambda e: e.tensor_tensor(out=oo[:, pb_, :], in0=ot[:, 0, :], in1=ot[:, 1, :], op=ALU.add),
                  reads=[t_ot[0], t_ot[1]], writes=[t_oo[pb_]])
            S_.op(pe_, lambda e: e.tensor_tensor(out=osq[:, pb_, :], in0=oo[:, pb_, :], in1=oo[:, pb_, :], op=ALU.mult),
                  reads=[t_oo[pb_]], writes=[t_osq[pb_]])
            deferred.append([3, lambda: epilogue_part2(h, qb, pb_)])

        def retire(p):
            emit_av(p)
            slots, h, kc, first, last = p
            if last:
                epilogue_part1(h, p_qb[id(p)], p_blk[id(p)])

        p_qb = {}
        p_blk = {}
        pend = []
        head_loads(0)
        for (h, qb, kc, first, last, blk) in steps:
            hb = h % 2
            pbuf = (h // 2) % 2
            off = (h % 2) * HD
            if first and qb == 0:
                hstep = 0
            hstep += 1
            if hstep == LAG + 2 and h + 1 < H:
                head_loads(h + 1)
            sl = []
            sbanks = [psS[P.ring("psS", 3)] for m in range(2)]

            def sc(e, kc=kc, qb=qb, sbanks=sbanks, hb=hb):
                for m in range(2):
                    ins = e.matmul(sbanks[m][0][:], lhsT=Kh[m * HD:(m + 1) * HD, hb, kc * 128:(kc + 1) * 128],
                                   rhs=Qh[m * HD:(m + 1) * HD, hb, qb * TB:(qb + 1) * TB], start=True, stop=True)
                return ins
            S_.op("pe", sc, reads=t_Kh[hb] + t_Qh[hb], writes=[sbanks[0][1], sbanks[1][1]])
            if len(pend) >= LAG:
                retire(pend.pop(0))
            if deferred:
                deferred[0][0] -= 1
                if deferred[0][0] <= 0:
                    deferred.pop(0)[1]()
            a0 = OFF + qb * TB - kc * 128
            for m in range(2):
                r = P.ring("eb", NE)
                s_ = P.ring("ET", NET)
                sl.append(s_)
                S_.op("act", lambda e, r=r, bk=sbanks[m][0]: e.activation(out=eb[:, r, :], in_=bk[:], func=AF.Exp, scale=SCALE),
                      reads=[sbanks[m][1]], writes=[t_eb[r]])
                eng = "pool" if (P.ring("mulsel", 5) in (1, 3) and not (pool_busy and blk < NBUSY)) else "dve"
                S_.op(eng, lambda e, r=r, s_=s_, a0=a0, hb=hb: e.tensor_tensor(out=ET[:, s_, :], in0=eb[:, r, :],
                                                                              in1=Dh[:, hb, a0:a0 + TB], op=ALU.mult),
                      reads=[t_eb[r], t_Dh[hb]], writes=[t_ET[s_]])
            item = (sl, h, kc, first, last)
            p_qb[id(item)] = qb
            p_blk[id(item)] = blk
            pend.append(item)
        while pend:
            retire(pend.pop(0))
        while deferred:
            deferred.pop(0)[1]()
        P.finish()

        P = Phase(nc, "pC")
        S_ = P.S
        xt, t_xt = P.sb("xt", [128, 4, D], F32, nT=4)
        x1T, t_x1 = P.sb("x1T", [128, NCH, TB], F32, nT=NCH)
        hT, t_hT = P.sb("hT", [128, NCH, TB], BF16, nT=NCH)
        aT, t_aT = P.sb("aT", [128, NCH, TB], BF16, nT=NCH)
        rtmp, t_rtmp = P.sb("rtmp", [128, 4, TB], F32, nT=4)
        sqb, t_sqb = P.sb("sqb", [128, 4, TB], BF16, nT=4)
        rs, t_rs = P.sb("rs", [128, TB], F32)
        tmpA, t_tmpA = P.sb("tmpA", [128, 2, 4, TB], F32, nT=2)
        sig, t_sig = P.sb("sig", [128, 2, TB], F32, nT=2)
        glu, t_glu = P.sb("glu", [128, 2, 4, TB], BF16, nT=2)
        pst = [P.ps("pst%d" % i) for i in range(2)]
        banks = [P.ps("pso%d" % i) for i in range(4)]
        psn, t_psn = P.ps("psn")
        t_scr = T("scrC")
        pw1_order = [0, 4, 1, 5, 2, 6, 3, 7]
        plan = []
        for tb in range(NB):
            plan += [wb_o[j] for j in range(4)] + mlp_plan(0) + [wb_pw1[j] for j in pw1_order]
        ws = WStream(P, plan)
        for tb in range(NB):
            for tc in range(4):
                r0 = tb * TB + tc * 128
                S_.dma("sp", lambda e, tc=tc, r0=r0: e.dma_start(out=xt[:, tc, :], in_=x[r0:r0 + 128, :]), t_xt[tc],
                       writes=[t_xt[tc]])
            S_.dma("sp", lambda e, tb=tb: e.dma_start(out=aT[:], in_=OT[:, tb * TB:(tb + 1) * TB].rearrange("(c p) t -> p c t", p=128)),
                   t_aT[0], writes=t_aT)
            ws.prefetch()
            for c in range(NCH):
                pb, t_pb = pst[c % 2]

                def trf(e, c=c, pb=pb):
                    for tc in range(4):
                        ins = e.transpose(pb[:, tc * 128:(tc + 1) * 128], xt[:, tc, c * 128:(c + 1) * 128], identF[:])
                    return ins
                S_.op("pe", trf, reads=t_xt, writes=[t_pb])
                if c % 2 == 0:
                    S_.op("act", lambda e, c=c, pb=pb: e.activation(out=x1T[:, c, :], in_=pb[:], func=AF.Copy), reads=[t_pb],
                          writes=[t_x1[c]])
                else:
                    S_.op("dve", lambda e, c=c, pb=pb: e.tensor_copy(out=x1T[:, c, :], in_=pb[:]), reads=[t_pb], writes=[t_x1[c]])

            def evac_o(ti, mc, bank, t_bank):
                dc = ti * 4 + mc
                S_.op("dve", lambda e: e.tensor_tensor(out=x1T[:, dc, :], in0=x1T[:, dc, :], in1=bank[:], op=ALU.add),
                      reads=[t_bank, t_x1[dc]], writes=[t_x1[dc]])
            dense(P, ws, aT, t_aT, 4, banks, evac_o)
            norm_fm(P, x1T, t_x1, CG_M0, hT, t_hT, psn, t_psn, rs, t_rs, sqb, t_sqb)
            mlp(P, ws, x1T, t_x1, hT, t_hT, aT, t_aT, banks, rtmp, t_rtmp)
            S_.dma("sp", lambda e, tb=tb: e.dma_start(out=X2T[:, tb * TB:(tb + 1) * TB].rearrange("(c p) t -> p c t", p=128), in_=x1T[:]),
                   t_x1[0], reads=t_x1, writes=[t_scr])
            norm_fm(P, x1T, t_x1, CG_C, hT, t_hT, psn, t_psn, rs, t_rs, sqb, t_sqb)

            def evac_pw1(ti, mc, bank, t_bank, tb=tb):
                j = pw1_order[ti]
                pb_ = (ti // 2) % 2
                if j < 4:
                    cc = j * 4 + mc
                    S_.op("act", lambda e: e.activation(out=tmpA[:, pb_, mc, :], in_=bank[:], func=AF.Identity, bias=col(CB_PW1, cc)),
                          reads=[t_bank], writes=[t_tmpA[pb_]])
                else:
                    cc = j * 4 + mc
                    r = P.ring("sig", 2)
                    S_.op("act", lambda e: e.activation(out=sig[:, r, :], in_=bank[:], func=AF.Sigmoid, bias=col(CB_PW1, cc)),
                          reads=[t_bank], writes=[t_sig[r]])
                    eng = "dve" if mc % 2 == 0 else "pool"
                    S_.op(eng, lambda e: e.tensor_tensor(out=glu[:, pb_, mc, :], in0=tmpA[:, pb_, mc, :], in1=sig[:, r, :], op=ALU.mult),
                          reads=[t_tmpA[pb_], t_sig[r]], writes=[t_glu[pb_]])
                    if mc == 3:
                        c0 = (j - 4) * 4 * 128
                        dst = GLU[c0:c0 + 512, tb * TB:(tb + 1) * TB].rearrange("(c p) t -> p c t", p=128)
                        S_.dma("sp", lambda e: e.dma_start(out=dst, in_=glu[:, pb_]), t_glu[pb_], reads=[t_glu[pb_]], writes=[t_scr])
            dense(P, ws, hT, t_hT, 8, banks, evac_pw1)
        P.finish()

        P = Phase(nc, "pD")
        S_ = P.S
        P.bgq = []
        P.bgn = 4
        ystg, t_ystg = P.sb("ystg", [128, 2, D], F32, nT=2)
        x1T, t_x1 = P.sb("x1T", [128, NCH, TB], F32, nT=NCH)
        hT, t_hT = P.sb("hT", [128, NCH, TB], BF16, nT=NCH)
        aT, t_aT = P.sb("aT", [128, NCH, TB], BF16, nT=NCH)
        gl, t_gl = P.sb("gl", [128, NCH, TB + 2 * CPAD], BF16)
        vb, t_vb = P.sb("vb", [128, NCH, TB], BF16, nT=NCH)
        cacc, t_cacc = P.sb("cacc", [128, 2, TB], F32, nT=2)
        rtmp, t_rtmp = P.sb("rtmp", [128, 4, TB], F32, nT=4)
        sqb, t_sqb = P.sb("sqb", [128, 4, TB], BF16, nT=4)
        rs, t_rs = P.sb("rs", [128, TB], F32)
        mu, t_mu = P.sb("mu", [128, TB], F32)
        nmr, t_nmr = P.sb("nmr", [128, TB], F32)
        pst = [P.ps("pst%d" % i) for i in range(2)]
        banks = [P.ps("pso%d" % i) for i in range(4)]
        psn, t_psn = P.ps("psn")
        psm, t_psm = P.ps("psm")
        t_scr = T("scrD")
        plan = []
        for tb in range(NB):
            plan += [wb_pw2[j] for j in range(4)] + mlp_plan(1)
        ws = WStream(P, plan)

        def load_gl(tb):
            t0 = tb * TB
            lo = max(t0 - CPAD, 0)
            hi = min(t0 + TB + CPAD, S)
            if lo > t0 - CPAD:
                S_.op("pool", lambda e: e.memset(gl[:, :, 0:CPAD], 0.0), writes=[t_gl])
            if hi < t0 + TB + CPAD:
                S_.op("pool", lambda e: e.memset(gl[:, :, TB + CPAD:TB + 2 * CPAD], 0.0), writes=[t_gl])
            o0 = lo - (t0 - CPAD)
            S_.dma("sp", lambda e: e.dma_start(out=gl[:, :, o0:o0 + hi - lo],
                                               in_=GLU[:, lo:hi].rearrange("(c p) t -> p c t", p=128)), t_gl, writes=[t_gl])

        def conv_ops():
            ops = []
            for c0 in range(0, NCH, 2):
                for k in range(CW):
                    for c in (c0, c0 + 1):
                        r = c % 2

                        def f(c=c, k=k, r=r):
                            if k == 0:
                                S_.op("dve", lambda e: e.tensor_scalar(out=cacc[:, r, :], in0=gl[:, c, 0:TB], scalar1=wdw[:, c, 0:1],
                                                                       scalar2=col(CB_DW, c), op0=ALU.mult, op1=ALU.add),
                                      reads=[t_gl], writes=[t_cacc[r]])
                            elif k < CW - 1:
                                S_.op("dve", lambda e: e.scalar_tensor_tensor(out=cacc[:, r, :], in0=gl[:, c, k:k + TB],
                                                                              scalar=wdw[:, c, k:k + 1], in1=cacc[:, r, :],
                                                                              op0=ALU.mult, op1=ALU.add),
                                      reads=[t_gl, t_cacc[r]], writes=[t_cacc[r]])
                            else:
                                S_.op("dve", lambda e: e.scalar_tensor_tensor(out=vb[:, c, :], in0=gl[:, c, k:k + TB],
                                                                              scalar=wdw[:, c, k:k + 1], in1=cacc[:, r, :],
                                                                              op0=ALU.mult, op1=ALU.add),
                                      reads=[t_gl, t_cacc[r]], writes=[t_vb[c]])
                        ops.append(f)
            return ops

        load_gl(0)
        ctmp, t_ctmp = P.sb("ctmp", [128, 2, TB], F32, nT=2)
        cacp, t_cacp = P.sb("cacp", [128, 2, TB], F32, nT=2)
        NDV = 14

        def conv_ops_pool(c0):
            ops = []
            for k in range(CW):
                for c in (c0, c0 + 1):
                    r = c % 2

                    def f(c=c, k=k, r=r):
                        if k == 0:
                            S_.op("act", lambda e: e.activation(out=cacp[:, r, :], in_=gl[:, c, 0:TB], func=AF.Identity,
                                                                scale=wdw[:, c, 0:1], bias=col(CB_DW, c)),
                                  reads=[t_gl], writes=[t_cacp[r]])
                            return
                        S_.op("act", lambda e: e.activation(out=ctmp[:, r, :], in_=gl[:, c, k:k + TB], func=AF.Copy,
                                                            scale=wdw[:, c, k:k + 1]), reads=[t_gl], writes=[t_ctmp[r]])
                        if k < CW - 1:
                            S_.op("pool", lambda e: e.tensor_tensor(out=cacp[:, r, :], in0=cacp[:, r, :], in1=ctmp[:, r, :], op=ALU.add),
                                  reads=[t_cacp[r], t_ctmp[r]], writes=[t_cacp[r]])
                        else:
                            S_.op("pool", lambda e: e.tensor_tensor(out=vb[:, c, :], in0=cacp[:, r, :], in1=ctmp[:, r, :], op=ALU.add),
                                  reads=[t_cacp[r], t_ctmp[r]], writes=[t_vb[c]])
                    ops.append(f)
            return ops

        dv_ops = [f for f in conv_ops()][:NDV * CW]
        pl_ops = []
        for c0 in range(NDV, NCH, 2):
            pl_ops += conv_ops_pool(c0)
        while dv_ops or pl_ops:
            if dv_ops:
                dv_ops.pop(0)()
            if pl_ops:
                pl_ops.pop(0)()
        for tb in range(NB):
            S_.dma("sp", lambda e, tb=tb: e.dma_start(out=x1T[:], in_=X2T[:, tb * TB:(tb + 1) * TB].rearrange("(c p) t -> p c t", p=128)),
                   t_x1[0], writes=t_x1)
            ws.prefetch()
            for c in range(NCH):
                S_.op("pe", lambda e, c=c: e.matmul(psm[:], lhsT=onesB[:], rhs=vb[:, c, :], start=(c == 0), stop=(c == NCH - 1)),
                      reads=[t_vb[c]], writes=[t_psm])
                r2 = P.ring("sqb", 4)
                eng = "pool" if c % 2 == 0 else "act"
                if eng == "pool":
                    S_.op("pool", lambda e, c=c, r2=r2: e.tensor_tensor(out=sqb[:, r2, :], in0=vb[:, c, :], in1=vb[:, c, :], op=ALU.mult),
                          reads=[t_vb[c]], writes=[t_sqb[r2]])
                else:
                    S_.op("act", lambda e, c=c, r2=r2: e.activation(out=sqb[:, r2, :], in_=vb[:, c, :], func=AF.Square),
                          reads=[t_vb[c]], writes=[t_sqb[r2]])
                S_.op("pe", lambda e, c=c, r2=r2: e.matmul(psn[:], lhsT=onesB[:], rhs=sqb[:, r2, :], start=(c == 0), stop=(c == NCH - 1)),
                      reads=[t_sqb[r2]], writes=[t_psn])
            S_.op("dve", lambda e: e.tensor_scalar(out=mu[:], in0=psm[:], scalar1=1.0 / D, scalar2=None, op0=ALU.mult),
                  reads=[t_psm], writes=[t_mu])
            S_.op("dve", lambda e: e.tensor_tensor(out=nmr[:], in0=mu[:], in1=mu[:], op=ALU.mult), reads=[t_mu], writes=[t_nmr])
            S_.op("dve", lambda e: e.scalar_tensor_tensor(out=rs[:], in0=psn[:], scalar=1.0 / D, in1=nmr[:], op0=ALU.mult,
                                                          op1=ALU.subtract), reads=[t_psn, t_nmr], writes=[t_rs])
            S_.op("dve", lambda e: e.tensor_scalar(out=rs[:], in0=rs[:], scalar1=LN_EPS, scalar2=None, op0=ALU.add),
                  reads=[t_rs], writes=[t_rs])
            S_.op("act", lambda e: e.activation(out=rs[:], in_=rs[:], func=AF.Ln), reads=[t_rs], writes=[t_rs])
            S_.op("act", lambda e: e.activation(out=rs[:], in_=rs[:], func=AF.Exp, scale=-0.5), reads=[t_rs], writes=[t_rs])
            S_.op("dve", lambda e: e.scalar_tensor_tensor(out=nmr[:], in0=mu[:], scalar=-1.0, in1=rs[:], op0=ALU.mult, op1=ALU.mult),
                  reads=[t_mu, t_rs], writes=[t_nmr])
            for c in range(NCH):
                eng = "dve" if c % 2 == 0 else "pool"
                r = P.ring("rtmp", 4)
                S_.op(eng, lambda e, c=c, r=r: e.tensor_tensor(out=rtmp[:, r, :], in0=vb[:, c, :], in1=rs[:], op=ALU.mult),
                      reads=[t_rs, t_vb[c]], writes=[t_rtmp[r]])
                S_.op(eng, lambda e, c=c, r=r: e.tensor_tensor(out=rtmp[:, r, :], in0=rtmp[:, r, :], in1=nmr[:], op=ALU.add),
                      reads=[t_nmr, t_rtmp[r]], writes=[t_rtmp[r]])
                S_.op("act", lambda e, c=c, r=r: e.activation(out=aT[:, c, :], in_=rtmp[:, r, :], func=AF.Silu, scale=col(CLN_G, c),
                                                             bias=col(CLN_B, c)), reads=[t_rtmp[r]], writes=[t_aT[c]])
            if tb + 1 < NB:
                load_gl(tb + 1)
                P.bgq = conv_ops()

            def evac_pw2(ti, mc, bank, t_bank):
                dc = ti * 4 + mc
                S_.op("dve", lambda e: e.scalar_tensor_tensor(out=x1T[:, dc, :], in0=bank[:], scalar=col(CB_PW2, dc), in1=x1T[:, dc, :],
                                                             op0=ALU.add, op1=ALU.add), reads=[t_bank, t_x1[dc]], writes=[t_x1[dc]])
            dense(P, ws, aT, t_aT, 4, banks, evac_pw2)
            norm_fm(P, x1T, t_x1, CG_M1, hT, t_hT, psn, t_psn, rs, t_rs, sqb, t_sqb)
            mlp(P, ws, x1T, t_x1, hT, t_hT, aT, t_aT, banks, rtmp, t_rtmp)
            while P.bgq:
                P.bgq.pop(0)()
            norm_fm(P, x1T, t_x1, CG_F, x1T, t_x1, psn, t_psn, rs, t_rs, sqb, t_sqb)
            for tc in range(4):
                yb = P.ring("ystg", 2)
                for dq in range(4):
                    pb, t_pb = pst[(tc * 4 + dq) % 2]

                    def trb(e, tc=tc, dq=dq, pb=pb):
                        for i in range(4):
                            c = dq * 4 + i
                            ins = e.transpose(pb[:, i * 128:(i + 1) * 128], x1T[:, c, tc * 128:(tc + 1) * 128], identF[:])
                        return ins
                    S_.op("pe", trb, reads=[t_x1[dq * 4 + i] for i in range(4)], writes=[t_pb])
                    if dq % 2 == 0:
                        S_.op("act", lambda e, yb=yb, dq=dq, pb=pb: e.activation(out=ystg[:, yb, dq * 512:(dq + 1) * 512], in_=pb[:], func=AF.Copy),
                              reads=[t_pb], writes=[t_ystg[yb]])
                    else:
                        S_.op("dve", lambda e, yb=yb, dq=dq, pb=pb: e.tensor_copy(out=ystg[:, yb, dq * 512:(dq + 1) * 512], in_=pb[:]),
                              reads=[t_pb], writes=[t_ystg[yb]])
                r0 = tb * TB + tc * 128
                S_.dma("sp", lambda e, yb=yb, r0=r0: e.dma_start(out=y[r0:r0 + 128, :], in_=ystg[:, yb, :]), t_ystg[yb],
                       reads=[t_ystg[yb]], writes=[t_scr])
        P.finish()
    return nc


_NC_CACHE = {}

WNAMES = ["attn_norm_g", "w_qkv", "lam_q1", "lam_k1", "lam_q2", "lam_k2", "subln_g", "w_o", "conv_norm_g",
          "conv_w_pw1", "conv_b_pw1", "conv_w_dw", "conv_b_dw", "conv_ln_g", "conv_ln_b", "conv_w_pw2",
          "conv_b_pw2", "mlp_norm_g", "w_up", "w_down", "final_norm_g"]


def prep_weights(inputs):
    f = lambda a: np.ascontiguousarray(np.asarray(a, dtype=np.float32))
    w = {}
    w["attn_norm_g"] = f(inputs["attn_norm_g"]).reshape(D)
    w["w_qkv"] = f(inputs["w_qkv"]).reshape(D, 3 * D)
    for n in ("lam_q1", "lam_k1", "lam_q2", "lam_k2"):
        w[n] = f(inputs[n]).reshape(1, HD)
    w["subln_g"] = f(inputs["subln_g"]).reshape(2 * HD)
    w["w_o"] = f(inputs["w_o"]).reshape(D, D)
    w["conv_norm_g"] = f(inputs["conv_norm_g"]).reshape(D)
    w["conv_w_pw1"] = f(inputs["conv_w_pw1"]).reshape(D, 2 * D)
    w["conv_b_pw1"] = f(inputs["conv_b_pw1"]).reshape(2 * D)
    w["conv_w_dw"] = f(inputs["conv_w_dw"]).reshape(CW, D)
    w["conv_b_dw"] = f(inputs["conv_b_dw"]).reshape(D)
    w["conv_ln_g"] = f(inputs["conv_ln_g"]).reshape(D)
    w["conv_ln_b"] = f(inputs["conv_ln_b"]).reshape(D)
    w["conv_w_pw2"] = f(inputs["conv_w_pw2"]).reshape(D, D)
    w["conv_b_pw2"] = f(inputs["conv_b_pw2"]).reshape(D)
    w["mlp_norm_g"] = f(inputs["mlp_norm_g"]).reshape(2, D)
    w["w_up"] = f(inputs["w_up"]).reshape(2, D, DFF)
    w["w_down"] = f(inputs["w_down"]).reshape(2, DFF, D)
    w["final_norm_g"] = f(inputs["final_norm_g"]).reshape(D)
    return w


def kernel(**inputs):
    xp = np.asarray(inputs["x_prompt"], dtype=np.float32)
    xs = np.asarray(inputs["x_sample"], dtype=np.float32)
    S = xp.shape[1]
    seqs = [xp[i] for i in range(xp.shape[0])] + [xs[i] for i in range(xs.shape[0])]
    n = len(seqs)
    if S not in _NC_CACHE:
        _NC_CACHE[S] = build(S)
    nc = _NC_CACHE[S]
    w = prep_weights(inputs)
    in_maps = []
    for i in range(n):
        m = dict(w)
        m["x"] = np.ascontiguousarray(seqs[i])
        in_maps.append(m)
    res = run_bass_kernel_spmd(nc, in_maps, core_ids=list(range(n)))
    outs = [np.asarray(r["y"], dtype=np.float32) for r in res.results]
    yp = np.stack(outs[:xp.shape[0]], axis=0)
    ys = np.stack(outs[xp.shape[0]:], axis=0)
    return (yp, ys)
```

```python
import math
from contextlib import ExitStack

import numpy as np
import concourse.bass as bass
import concourse.mybir as mybir
from concourse.bass_utils import run_bass_kernel_spmd

F32 = mybir.dt.float32
BF16 = mybir.dt.bfloat16
AF = mybir.ActivationFunctionType
ALU = mybir.AluOpType
AX = mybir.AxisListType

D = 2048
H = 16
HD = 64
DFF = 8192
TB = 512
NCH = 16
CW = 31
CPAD = 15
RMS_EPS = 1e-6
SUBLN_EPS = 1e-5
LN_EPS = 1e-5
SCALE = HD ** -0.5
LAMBDA_INIT0 = 0.8 - 0.6 * math.exp(-0.3 * 0)
SKIP_THRESH = 60.0
CASTS_IN_A = True
CASTS_IN_B = True
NBUSY = 40


class T:
    __slots__ = ("name", "w", "r", "sem", "ndma", "_lastdma")

    def __init__(self, name):
        self.name = name
        self.w = None
        self.r = []
        self.sem = None
        self.ndma = 0
        self._lastdma = None


class Sched:
    ENGS = ("pe", "act", "dve", "pool", "sp")

    def __init__(self, nc):
        self.nc = nc
        self.ops = []

    def _add(self, eng, fn, reads, writes, is_dma=False, grp=None):
        oid = len(self.ops)
        deps = set()
        for t in reads:
            if t.w is not None:
                deps.add(t.w)
        for t in writes:
            if t.w is not None:
                deps.add(t.w)
            deps.update(t.r)
        if is_dma:
            if grp._lastdma is not None:
                deps.add(grp._lastdma)
            grp._lastdma = oid
        for t in reads:
            t.r.append(oid)
        for t in writes:
            t.w = oid
            t.r = []
        deps.discard(oid)
        self.ops.append([eng, fn, deps, is_dma, grp, False, 0])
        return oid

    def op(self, eng, fn, reads=(), writes=()):
        return self._add(eng, fn, reads, writes)

    def dma(self, eng, fn, grp, reads=(), writes=()):
        return self._add(eng, fn, reads, writes, True, grp)

    def emit(self, stack):
        nc = self.nc
        ops = self.ops
        for o in ops:
            eng, is_dma = o[0], o[3]
            for d in o[2]:
                p = ops[d]
                if p[3] or is_dma or p[0] != eng or eng != "pe":
                    p[5] = True
        cnt = {e: 0 for e in self.ENGS}
        grps = []
        for o in ops:
            if o[3]:
                g = o[4]
                if g.sem is None:
                    g.sem = "pending"
                    grps.append(g)
                g.ndma += 1
                o[6] = 16 * g.ndma
                o[5] = True
            elif o[5]:
                cnt[o[0]] += 1
                o[6] = cnt[o[0]]
        esem = {e: stack.enter_context(nc.semaphore("s_" + e)) for e in self.ENGS}
        for i, g in enumerate(grps):
            g.sem = stack.enter_context(nc.semaphore("g%d" % i))
        self.nsem = len(grps) + 5
        per_eng = {e: [] for e in self.ENGS}
        for i, o in enumerate(ops):
            per_eng[o[0]].append(i)
        block = stack.enter_context(nc.Block())

        def run_engine(e, handle):
            waited = {}
            for i in per_eng[e]:
                eng, fn, deps, is_dma, grp, sig, count = ops[i]
                need = {}
                for d in deps:
                    p = ops[d]
                    if p[3]:
                        s = p[4].sem
                    elif is_dma or p[0] != eng or eng != "pe":
                        s = esem[p[0]]
                    else:
                        continue
                    k = id(s)
                    if need.get(k, (None, 0))[1] < p[6]:
                        need[k] = (s, p[6])
                for k, (s, v) in need.items():
                    if waited.get(k, 0) < v:
                        handle.wait_ge(s, v)
                        waited[k] = v
                ins = fn(handle)
                if sig:
                    if is_dma:
                        ins.then_inc(grp.sem, 16)
                    else:
                        ins.then_inc(esem[eng], 1)

        @block.tensor
        def _(h):
            run_engine("pe", h)

        @block.scalar
        def _(h):
            run_engine("act", h)

        @block.vector
        def _(h):
            run_engine("dve", h)

        @block.gpsimd
        def _(h):
            run_engine("pool", h)

        @block.sync
        def _(h):
            run_engine("sp", h)


class Phase:
    def __init__(self, nc, name):
        self.nc = nc
        self.name = name
        self.stack = ExitStack()
        self.S = Sched(nc)
        self.tiles = []
        self._rr = {}

    def T(self, name):
        t = T(name)
        self.tiles.append(t)
        return t

    def sb(self, name, shape, dt, nT=1):
        ap = self.stack.enter_context(self.nc.sbuf_tensor(self.name + "_" + name, shape, dt))
        if nT == 1:
            return ap, self.T(name)
        return ap, [self.T(name + str(i)) for i in range(nT)]

    def ps(self, name):
        ap = self.stack.enter_context(self.nc.psum_tensor(self.name + "_" + name, [128, 512], F32))
        return ap, self.T(name)

    def ring(self, key, n):
        i = self._rr.get(key, 0)
        self._rr[key] = i + 1
        return i % n

    def finish(self):
        S = self.S
        tb = T("bar")
        S.op("sp", lambda e: e.nop(), reads=(), writes=self.tiles + [tb])
        for eng in ("pe", "act", "dve", "pool"):
            S.op(eng, lambda e: e.nop(), reads=[tb])
        S.emit(self.stack)
        self.stack.close()


class WStream:
    def __init__(self, P, plan, nslots=3, shape=(128, NCH, 512), name="w"):
        self.P = P
        self.plan = plan
        self.n = nslots
        self.slots = []
        for i in range(nslots):
            ap, t = P.sb("%sr%d" % (name, i), list(shape), BF16)
            self.slots.append((ap, t))
        self.issued = 0
        self.used = 0

    def _issue(self):
        i = self.issued
        ap, t = self.slots[i % self.n]
        src = self.plan[i]
        self.P.S.dma("sp", lambda e, ap=ap, src=src: e.dma_start(out=ap[:], in_=src), t, writes=[t])
        self.issued += 1

    def next(self):
        while self.issued < len(self.plan) and self.issued < self.used + self.n:
            self._issue()
        ap, t = self.slots[self.used % self.n]
        self.used += 1
        return ap, t

    def prefetch(self):
        while self.issued < len(self.plan) and self.issued < self.used + self.n:
            self._issue()


def alibi_slope(h):
    return 2.0 ** (-8.0 * (h + 1) / H)


def build(S, dbg=False):
    NB = S // TB
    NKC = S // 128
    nc = bass.Bass("TRN2", target_bir_lowering=False)

    def din(name, shape):
        return nc.dram_tensor(name, list(shape), F32, kind="ExternalInput").ap()

    x = din("x", [S, D])
    attn_norm_g = din("attn_norm_g", [D])
    w_qkv = din("w_qkv", [D, 3 * D])
    lam_in = [din(n, [1, HD]) for n in ("lam_q1", "lam_k1", "lam_q2", "lam_k2")]
    subln_g = din("subln_g", [2 * HD])
    w_o = din("w_o", [D, D])
    conv_norm_g = din("conv_norm_g", [D])
    w_pw1 = din("conv_w_pw1", [D, 2 * D])
    b_pw1 = din("conv_b_pw1", [2 * D])
    w_dw = din("conv_w_dw", [CW, D])
    b_dw = din("conv_b_dw", [D])
    ln_g = din("conv_ln_g", [D])
    ln_b = din("conv_ln_b", [D])
    w_pw2 = din("conv_w_pw2", [D, D])
    b_pw2 = din("conv_b_pw2", [D])
    mlp_norm_g = din("mlp_norm_g", [2, D])
    w_up = din("w_up", [2, D, DFF])
    w_down = din("w_down", [2, DFF, D])
    final_norm_g = din("final_norm_g", [D])
    y = nc.dram_tensor("y", [S, D], F32, kind="ExternalOutput").ap()

    def scratch(name, shape, dt):
        return nc.dram_tensor(name, list(shape), dt, **skind).ap()

    skind = dict(kind="ExternalOutput") if dbg else {}
    wb_qkv = nc.dram_tensor("wb_qkv", [12, 128, NCH, 512], BF16).ap()
    wb_o = nc.dram_tensor("wb_o", [4, 128, NCH, 512], BF16).ap()
    wb_pw1 = nc.dram_tensor("wb_pw1", [8, 128, NCH, 512], BF16).ap()
    wb_pw2 = nc.dram_tensor("wb_pw2", [4, 128, NCH, 512], BF16, **skind).ap()
    wb_up = nc.dram_tensor("wb_up", [2, 16, 128, NCH, 512], BF16, **skind).ap()
    wb_down = nc.dram_tensor("wb_down", [2, 16, 128, NCH, 512], BF16, **skind).ap()
    DIAG = nc.dram_tensor("diag", [NCH, 128, CW, 128], BF16).ap()
    QT = scratch("QT", [H, 128, S], BF16)
    KT = scratch("KT", [H, 128, S], BF16)
    Vs = scratch("Vs", [S, D], BF16)
    OT = scratch("OT", [D, S], BF16)
    X2T = scratch("X2T", [D, S], F32)
    GLU = scratch("GLU", [D, S], BF16)

    outer = ExitStack()
    with outer:
        def psb(name, shape, dt):
            return outer.enter_context(nc.sbuf_tensor(name, shape, dt))

        NCST = 448
        cst = psb("cst", [128, NCST], F32)
        wdw = psb("wdwc", [128, NCH, 32], F32)
        identF = psb("identF", [128, 128], F32)
        identB = psb("identB", [128, 128], BF16)
        onesB = psb("onesB", [128, 128], BF16)
        CG_A, CG_C, CG_M0, CG_M1, CG_F, CB_PW1, CB_DW, CLN_G, CLN_B, CB_PW2, CSUB = [32 * i for i in range(11)]
        C_NLAM = 352 + 1
        C_GSUB = 352 + 2

        def col(base, c):
            return cst[:, base + c:base + c + 1]

        P = Phase(nc, "p0")
        S_ = P.S
        stage, _ = P.sb("stage", [32, 11 * 128], F32)
        stage2, t_stage2 = P.sb("stage2", [CW, D], F32)
        lamv, t_lamv = P.sb("lamv", [128, 4, HD], F32)
        ltmp, t_ltmp = P.sb("ltmp", [128, 2 * HD + 8], F32)
        dg, t_dg = P.sb("dg", [128, 2, CW, 128], BF16, nT=2)
        ps0, t_ps0 = P.ps("ps0")
        ps1, t_ps1 = P.ps("ps1")
        t_cst = P.T("cst")
        t_wdw = P.T("wdw")
        t_idF = P.T("idF")
        t_idB = P.T("idB")
        t_ones = P.T("ones")

        NG = 4
        cgrp = [P.T("cg%d" % i) for i in range(NG)]
        t_wbdram = T("wbdram")

        def cast(dst, src):
            g = cgrp[P.ring("cg", NG)]
            S_.dma("pool", lambda e, dst=dst, src=src: e.dma_start(out=dst, in_=src), g, writes=[g])

        def cast_std(dst_tiles, src2d, kgroups, ncolblk, col0=0):
            for g in range(kgroups):
                for j in range(ncolblk):
                    src = src2d[g * 2048:(g + 1) * 2048, col0 + j * 512:col0 + (j + 1) * 512]
                    cast(dst_tiles[g * ncolblk + j], src.rearrange("(c p) m -> p c m", p=128))

        cast_std([wb_qkv[j] for j in range(12)], w_qkv, 1, 12)

        S_.op("pool", lambda e: e.memset(identF[:], 0.0), writes=[t_idF])
        S_.op("pool", lambda e: e.affine_select(out=identF[:], in_=identF[:], pattern=[[-1, 128]], compare_op=ALU.not_equal,
                                                fill=1.0, base=0, channel_multiplier=1), reads=[t_idF], writes=[t_idF])
        S_.op("pool", lambda e: e.tensor_copy(out=identB[:], in_=identF[:]), reads=[t_idF], writes=[t_idB])
        S_.op("pool", lambda e: e.memset(onesB[:], 1.0), writes=[t_ones])

        vecs = [(attn_norm_g, 16), (conv_norm_g, 16), (mlp_norm_g[0], 16), (mlp_norm_g[1], 16), (final_norm_g, 16),
                (b_pw1, 32), (b_dw, 16), (ln_g, 16), (ln_b, 16), (b_pw2, 16), (subln_g, 1)]
        t_stage = []
        for i, (v, n) in enumerate(vecs):
            ts = P.T("stg%d" % i)
            t_stage.append(ts)
            S_.dma("sp", lambda e, i=i, v=v, n=n: e.dma_start(out=stage[0:n, i * 128:(i + 1) * 128],
                                                           in_=v.rearrange("(c p) -> c p", p=128)), ts, writes=[ts])

        def tr_vecs(e):
            for i, (v, n) in enumerate(vecs):
                ins = e.transpose(ps0[:, i * 32:i * 32 + n], stage[0:n, i * 128:(i + 1) * 128], identF[0:n, 0:n])
            return ins
        S_.op("pe", tr_vecs, reads=t_stage + [t_idF], writes=[t_ps0])
        S_.op("dve", lambda e: e.tensor_copy(out=cst[:, 0:352], in_=ps0[:, 0:352]), reads=[t_ps0], writes=[t_cst])
        S_.dma("sp", lambda e: e.dma_start(out=stage2[:], in_=w_dw), t_stage2, writes=[t_stage2])

        def tr_wdw(e):
            for c in range(NCH):
                ins = e.transpose(ps1[:, c * 32:c * 32 + CW], stage2[0:CW, c * 128:(c + 1) * 128], identF[0:CW, 0:CW])
            return ins
        S_.op("pe", tr_wdw, reads=[t_stage2, t_idF], writes=[t_ps1])
        S_.op("dve", lambda e: e.tensor_copy(out=wdw[:, :, 0:CW], in_=ps1[:].rearrange("p (c k) -> p c k", k=32)[:, :, 0:CW]),
              reads=[t_ps1], writes=[t_wdw])
        for i in range(4):
            S_.dma("sp", lambda e, i=i: e.dma_start(out=lamv[:, i, :], in_=lam_in[i].partition_broadcast(128)),
                   t_lamv, writes=[t_lamv])

        S_.op("dve", lambda e: e.tensor_tensor(out=ltmp[:, 0:HD], in0=lamv[:, 0, :], in1=lamv[:, 1, :], op=ALU.mult),
              reads=[t_lamv], writes=[t_ltmp])
        S_.op("dve", lambda e: e.tensor_tensor(out=ltmp[:, HD:2 * HD], in0=lamv[:, 2, :], in1=lamv[:, 3, :], op=ALU.mult),
              reads=[t_lamv], writes=[t_ltmp])
        S_.op("dve", lambda e: e.tensor_reduce(out=ltmp[:, 2 * HD:2 * HD + 2], in_=ltmp[:, 0:2 * HD].rearrange("p (a b) -> p a b", a=2),
                                               axis=AX.X, op=ALU.add), reads=[t_ltmp], writes=[t_ltmp])
        S_.op("act", lambda e: e.activation(out=ltmp[:, 2 * HD + 2:2 * HD + 4], in_=ltmp[:, 2 * HD:2 * HD + 2], func=AF.Exp),
              reads=[t_ltmp], writes=[t_ltmp])
        S_.op("dve", lambda e: e.tensor_tensor(out=ltmp[:, 2 * HD + 4:2 * HD + 5], in0=ltmp[:, 2 * HD + 2:2 * HD + 3],
                                               in1=ltmp[:, 2 * HD + 3:2 * HD + 4], op=ALU.subtract), reads=[t_ltmp], writes=[t_ltmp])
        S_.op("dve", lambda e: e.tensor_scalar(out=cst[:, C_NLAM:C_NLAM + 1], in0=ltmp[:, 2 * HD + 4:2 * HD + 5], scalar1=LAMBDA_INIT0,
                                               scalar2=-1.0, op0=ALU.add, op1=ALU.mult), reads=[t_ltmp, t_cst], writes=[t_cst])
        S_.op("dve", lambda e: e.tensor_scalar(out=cst[:, C_GSUB:C_GSUB + 1], in0=cst[:, CSUB:CSUB + 1],
                                               scalar1=(1.0 - LAMBDA_INIT0), scalar2=None, op0=ALU.mult), reads=[t_cst], writes=[t_cst])

        def emit_casts(P, jobs, ngroups=4, reads=()):
            if not hasattr(P, "cgrp"):
                P.cgrp = [P.T("cg%d" % i) for i in range(ngroups)]
            for dst, src in jobs:
                g = P.cgrp[P.ring("cg", ngroups)]
                P.S.dma("pool", lambda e, dst=dst, src=src: e.dma_start(out=dst, in_=src), g, reads=list(reads), writes=[g])

        def cast_jobs(dst_tiles, src2d, kgroups, ncolblk):
            jobs = []
            for g in range(kgroups):
                for j in range(ncolblk):
                    src = src2d[g * 2048:(g + 1) * 2048, j * 512:(j + 1) * 512]
                    jobs.append((dst_tiles[g * ncolblk + j], src.rearrange("(c p) m -> p c m", p=128)))
            return jobs

        late_jobs = cast_jobs([wb_o[j] for j in range(4)], w_o, 1, 4)
        late_jobs += cast_jobs([wb_up[0, j] for j in range(16)], w_up[0], 1, 16)
        late_jobs += cast_jobs([wb_down[0, j] for j in range(16)], w_down[0], 4, 4)
        late_jobs += cast_jobs([wb_pw1[j] for j in range(8)], w_pw1, 1, 8)
        late_jobs2 = cast_jobs([wb_pw2[j] for j in range(4)], w_pw2, 1, 4)
        late_jobs2 += cast_jobs([wb_up[1, j] for j in range(16)], w_up[1], 1, 16)
        late_jobs2 += cast_jobs([wb_down[1, j] for j in range(16)], w_down[1], 4, 4)
        if not CASTS_IN_B:
            late_jobs += late_jobs2
            late_jobs2 = []

        if not CASTS_IN_A:
            emit_casts(P, late_jobs)
        P.finish()

        def rstd_from(P, src_ap, dst_ap, t_src, t_dst, mult, eps):
            S_ = P.S
            S_.op("dve", lambda e: e.tensor_scalar(out=dst_ap, in0=src_ap, scalar1=mult, scalar2=eps,
                                                   op0=ALU.mult, op1=ALU.add), reads=[t_src], writes=[t_dst])

            S_.op("act", lambda e: e.activation(out=dst_ap, in_=dst_ap, func=AF.Ln), reads=[t_dst], writes=[t_dst])
            S_.op("act", lambda e: e.activation(out=dst_ap, in_=dst_ap, func=AF.Exp, scale=-0.5), reads=[t_dst], writes=[t_dst])

        def dense(P, ws, act, t_act, ntiles, banks, evac, swap=False):
            S_ = P.S
            for ti in range(ntiles):
                w, t_w = ws.next()
                for mc in range(4):
                    bi = P.ring("bank", len(banks))
                    bank, t_bank = banks[bi]

                    def mm(e, w=w, mc=mc, bank=bank):
                        for c in range(NCH):
                            if swap:
                                ins = e.matmul(bank[:], lhsT=act[:, c, mc * 128:(mc + 1) * 128], rhs=w[:, c, :],
                                               start=(c == 0), stop=(c == NCH - 1))
                            else:
                                ins = e.matmul(bank[:], lhsT=w[:, c, mc * 128:(mc + 1) * 128], rhs=act[:, c, :],
                                               start=(c == 0), stop=(c == NCH - 1))
                        return ins
                    if ti == 0 and mc == 0:
                        for c in range(NCH):
                            def mm1(e, w=w, bank=bank, c=c):
                                if swap:
                                    return e.matmul(bank[:], lhsT=act[:, c, 0:128], rhs=w[:, c, :], start=(c == 0), stop=(c == NCH - 1))
                                return e.matmul(bank[:], lhsT=w[:, c, 0:128], rhs=act[:, c, :], start=(c == 0), stop=(c == NCH - 1))
                            S_.op("pe", mm1, reads=[t_w, t_act[c]], writes=[t_bank])
                    else:
                        S_.op("pe", mm, reads=[t_w] + list(t_act), writes=[t_bank])
                    evac(ti, mc, bank, t_bank)
                    for _ in range(getattr(P, "bgn", 0)):
                        if P.bgq:
                            P.bgq.pop(0)()

        def norm_fm(P, x1T, t_x1, gbase, outT, t_out, psn, t_psn, rs, t_rs, sqb, t_sqb):
            S_ = P.S
            for c in range(NCH):
                r = P.ring("sqb", len(t_sqb))
                eng = "pool" if c % 2 == 0 else "act"
                if eng == "pool":
                    S_.op("pool", lambda e, c=c, r=r: e.tensor_tensor(out=sqb[:, r, :], in0=x1T[:, c, :], in1=x1T[:, c, :],
                                                                     op=ALU.mult), reads=[t_x1[c]], writes=[t_sqb[r]])
                else:
                    S_.op("act", lambda e, c=c, r=r: e.activation(out=sqb[:, r, :], in_=x1T[:, c, :], func=AF.Square),
                          reads=[t_x1[c]], writes=[t_sqb[r]])
                S_.op("pe", lambda e, c=c, r=r: e.matmul(psn[:], lhsT=onesB[:], rhs=sqb[:, r, :], start=(c == 0),
                                                        stop=(c == NCH - 1)), reads=[t_sqb[r]], writes=[t_psn])
            rstd_from(P, psn[:], rs[:], t_psn, t_rs, 1.0 / D, RMS_EPS)
            for c in range(NCH):
                S_.op("dve", lambda e, c=c: e.scalar_tensor_tensor(out=outT[:, c, :], in0=x1T[:, c, :], scalar=col(gbase, c),
                                                                  in1=rs[:], op0=ALU.mult, op1=ALU.mult),
                      reads=[t_x1[c], t_rs], writes=[t_out[c]])

        def mlp(P, ws, x1T, t_x1, hT, t_hT, aT, t_aT, banks, rtmp, t_rtmp):
            S_ = P.S
            bg_save = getattr(P, "bgn", 0)
            for g in range(4):
                P.bgn = getattr(P, "bgn_up", bg_save)
                def evac_up(ti, mc, bank, t_bank):
                    fc = ti * 4 + mc
                    r = P.ring("rtmp", len(t_rtmp))
                    S_.op("act", lambda e, r=r, bank=bank: e.activation(out=rtmp[:, r, :], in_=bank[:], func=AF.Relu),
                          reads=[t_bank], writes=[t_rtmp[r]])
                    S_.op("pool", lambda e, r=r, fc=fc: e.tensor_tensor(out=aT[:, fc, :], in0=rtmp[:, r, :], in1=rtmp[:, r, :],
                                                                       op=ALU.mult), reads=[t_rtmp[r]], writes=[t_aT[fc]])
                dense(P, ws, hT, t_hT, 4, banks, evac_up)
                P.bgn = getattr(P, "bgn_dn", bg_save)

                def evac_dn(ti, mc, bank, t_bank):
                    dc = ti * 4 + mc
                    S_.op("dve", lambda e, dc=dc, bank=bank: e.tensor_tensor(out=x1T[:, dc, :], in0=x1T[:, dc, :], in1=bank[:],
                                                                            op=ALU.add), reads=[t_bank, t_x1[dc]], writes=[t_x1[dc]])
                dense(P, ws, aT, t_aT, 4, banks, evac_dn)
            P.bgn = bg_save

        def mlp_plan(l):
            plan = []
            for g in range(4):
                plan += [wb_up[l, 4 * g + jj] for jj in range(4)]
                plan += [wb_down[l, g * 4 + j] for j in range(4)]
            return plan

        P = Phase(nc, "pA")
        S_ = P.S
        pace, _ = P.sb("pace", [128, 16], F32)
        cast_chunks = []
        if CASTS_IN_A:
            per = (len(late_jobs) + NB - 1) // NB
            cast_chunks = [late_jobs[i * per:(i + 1) * per] for i in range(NB)]
        xt2, t_xt2 = P.sb("xt", [128, 2, 4, D], F32, nT=8)
        junk, t_junk = P.sb("junk", [128, D], BF16)
        ssq2, t_ssq2 = P.sb("ssq", [128, 2, 8], F32, nT=2)
        hT2, t_hT2 = P.sb("hT", [128, 2, NCH, TB], BF16, nT=2 * NCH)
        sqk, t_sqk = P.sb("sqk", [128, 2, 4, TB], BF16, nT=2)
        sv, t_sv = P.sb("sv", [128, 2, 4, 512], BF16, nT=2)
        pst = [P.ps("pst%d" % i) for i in range(2)]
        banks = [P.ps("pso%d" % i) for i in range(4)]
        t_scr = T("scrA")
        ws = WStream(P, [wb_qkv[j] for j in range(12)] * NB)

        def prep_front(tb):
            pb_ = tb % 2
            xt = xt2[:, pb_]
            t_xt = t_xt2[pb_ * 4:pb_ * 4 + 4]
            ssq = ssq2[:, pb_]
            t_ssq = t_ssq2[pb_]
            for tc in range(4):
                r0 = tb * TB + tc * 128
                S_.dma("sp", lambda e, tc=tc, r0=r0: e.dma_start(out=xt[:, tc, :], in_=x[r0:r0 + 128, :]), t_xt[tc],
                       writes=[t_xt[tc]])
            for tc in range(4):
                S_.op("act", lambda e, tc=tc: e.activation(out=junk[:], in_=xt[:, tc, :], func=AF.Square,
                                                          accum_out=ssq[:, tc:tc + 1]),
                      reads=[t_xt[tc]], writes=[t_junk, t_ssq])
            rstd_from(P, ssq[:, 0:4], ssq[:, 4:8], t_ssq, t_ssq, 1.0 / D, RMS_EPS)
            for tc in range(4):
                S_.op("dve", lambda e, tc=tc: e.tensor_scalar(out=xt[:, tc, :], in0=xt[:, tc, :], scalar1=ssq[:, 4 + tc:5 + tc],
                                                             scalar2=None, op0=ALU.mult), reads=[t_ssq, t_xt[tc]], writes=[t_xt[tc]])

        def prep_back(tb):
            pb_ = tb % 2
            xt = xt2[:, pb_]
            t_xt = t_xt2[pb_ * 4:pb_ * 4 + 4]
            hT = hT2[:, pb_]
            t_hT = t_hT2[pb_ * NCH:(pb_ + 1) * NCH]
            for c in range(NCH):
                pb, t_pb = pst[c % 2]

                def trf(e, c=c, pb=pb):
                    for tc in range(4):
                        ins = e.transpose(pb[:, tc * 128:(tc + 1) * 128], xt[:, tc, c * 128:(c + 1) * 128], identF[:])
                    return ins
                S_.op("pe", trf, reads=t_xt, writes=[t_pb])
                if c % 2 == 0:
                    S_.op("act", lambda e, c=c, pb=pb: e.activation(out=hT[:, c, :], in_=pb[:], func=AF.Identity, scale=col(CG_A, c)),
                          reads=[t_pb], writes=[t_hT[c]])
                else:
                    S_.op("dve", lambda e, c=c, pb=pb: e.tensor_scalar(out=hT[:, c, :], in0=pb[:], scalar1=col(CG_A, c), scalar2=None,
                                                                      op0=ALU.mult), reads=[t_pb], writes=[t_hT[c]])

        prep_front(0)
        ws.prefetch()
        prep_back(0)
        for tb in range(NB):
            hT = hT2[:, tb % 2]
            t_hT = t_hT2[(tb % 2) * NCH:(tb % 2 + 1) * NCH]
            if cast_chunks and cast_chunks[tb]:
                tp = P.T("pace%d" % tb)
                S_.op("dve", lambda e, tb=tb: e.memset(pace[:, tb % 16:tb % 16 + 1], 0.0), writes=[tp])
                emit_casts(P, cast_chunks[tb], reads=[tp])
            if tb + 1 < NB:
                prep_front(tb + 1)

            def evac_qk(ti, mc, bank, t_bank, tb=tb):
                b = ti % 2
                if mc % 2 == 0:
                    S_.op("act", lambda e: e.activation(out=sqk[:, b, mc, :], in_=bank[:], func=AF.Copy), reads=[t_bank],
                          writes=[t_sqk[b]])
                else:
                    S_.op("dve", lambda e: e.tensor_copy(out=sqk[:, b, mc, :], in_=bank[:]), reads=[t_bank], writes=[t_sqk[b]])
                if mc == 3:
                    jt = ti % 4
                    dst = (QT if ti < 4 else KT)[jt * 4:jt * 4 + 4, :, tb * TB:(tb + 1) * TB].rearrange("c p t -> p c t")
                    S_.dma("sp", lambda e: e.dma_start(out=dst, in_=sqk[:, b]), t_sqk[b], reads=[t_sqk[b]], writes=[t_scr])
            dense(P, ws, hT, t_hT, 8, banks, evac_qk)
            if tb + 1 < NB:
                prep_back(tb + 1)

            def evac_v(ti, mc, bank, t_bank, tb=tb):
                b = ti % 2
                if mc % 2 == 0:
                    S_.op("act", lambda e: e.activation(out=sv[:, b, mc, :], in_=bank[:], func=AF.Copy), reads=[t_bank],
                          writes=[t_sv[b]])
                else:
                    S_.op("dve", lambda e: e.tensor_copy(out=sv[:, b, mc, :], in_=bank[:]), reads=[t_bank], writes=[t_sv[b]])
                if mc == 3:
                    dst = Vs[tb * TB:(tb + 1) * TB, ti * 512:(ti + 1) * 512].rearrange("(tc p) e -> p tc e", p=128)
                    S_.dma("sp", lambda e: e.dma_start(out=dst, in_=sv[:, b]), t_sv[b], reads=[t_sv[b]], writes=[t_scr])
            dense(P, ws, hT, t_hT, 4, banks, evac_v, swap=True)
        P.finish()

        P = Phase(nc, "pB")
        S_ = P.S
        NU = 2 * S - 128
        OFF = S - 128
        AB, t_AB = P.sb("AB", [128, NU], F32)
        Dh, t_Dh = P.sb("Dh", [128, 2, NU], BF16, nT=2)
        Qh, _ = P.sb("Qh", [128, 2, S], BF16)
        Kh, _ = P.sb("Kh", [128, 2, S], BF16)
        t_Qh = [[P.T("Qh%d%d" % (b, m)) for m in range(2)] for b in range(2)]
        t_Kh = [[P.T("Kh%d%d" % (b, m)) for m in range(2)] for b in range(2)]
        Vh, t_Vh = P.sb("Vh", [128, 2, NKC, 128], BF16, nT=2)
        NE = 8
        eb, t_eb = P.sb("eb", [128, NE, 512], F32, nT=NE)
        NET = 10
        ET, t_ET = P.sb("ET", [128, NET, 512], BF16, nT=NET)
        rz, t_rz = P.sb("rz", [128, 2, 512], F32, nT=2)
        ot, t_ot = P.sb("ot", [128, 2, 512], F32, nT=2)
        oo, t_oo = P.sb("oo", [128, 2, 512], F32, nT=2)
        osq, t_osq = P.sb("osq", [128, 2, 512], BF16, nT=2)
        orr, t_orr = P.sb("orr", [128, 512], F32)
        on, t_on = P.sb("on", [128, 2, 512], BF16, nT=2)
        psS = [P.ps("psS%d" % i) for i in range(3)]
        psO = [P.ps("psO%d" % i) for i in range(2)]
        psZ = [P.ps("psZ%d" % i) for i in range(2)]
        psE, t_psE = P.ps("psE")
        t_scr = T("scrB")
        LAG = 3

        S_.op("pool", lambda e: e.iota(AB[:], pattern=[[1, NU]], base=-OFF, channel_multiplier=-1,
                                       allow_small_or_imprecise_dtypes=True), writes=[t_AB])
        S_.op("act", lambda e: e.activation(out=AB[:], in_=AB[:], func=AF.Abs), reads=[t_AB], writes=[t_AB])
        pool_busy = bool(late_jobs2)
        if late_jobs2:
            cgB = [P.T("cgB%d" % i) for i in range(4)]
            for dst, src in late_jobs2:
                g = cgB[P.ring("cgB", 4)]
                S_.dma("pool", lambda e, dst=dst, src=src: e.dma_start(out=dst, in_=src), g, writes=[g])
            late_jobs2 = []

        def needed(h, kc, qb):
            q0, k0 = qb * TB, kc * 128
            mind = max(k0 - (q0 + TB - 1), q0 - (k0 + 127), 0)
            return alibi_slope(h) * mind <= SKIP_THRESH

        deferred = []
        def epilogue_part2(h, qb, pb_):
            ob = P.ring("on", 2)
            S_.op("pe", lambda e: e.matmul(psE[:], lhsT=onesB[:], rhs=osq[:, pb_, :], start=True, stop=True), reads=[t_osq[pb_]],
                  writes=[t_psE])
            rstd_from(P, psE[:], orr[:], t_psE, t_orr, 1.0 / 128.0, SUBLN_EPS)
            S_.op("dve", lambda e: e.scalar_tensor_tensor(out=on[:, ob, :], in0=oo[:, pb_, :], scalar=cst[:, C_GSUB:C_GSUB + 1],
                                                          in1=orr[:], op0=ALU.mult, op1=ALU.mult),
                  reads=[t_oo[pb_], t_orr], writes=[t_on[ob]])
            S_.dma("sp", lambda e: e.dma_start(out=OT[h * 128:(h + 1) * 128, qb * TB:(qb + 1) * TB], in_=on[:, ob, :]), t_on[ob],
                   reads=[t_on[ob]], writes=[t_scr])

        def head_loads(h):
            hb = h % 2
            hp = h // 2
            pbuf = hp % 2
            slope = alibi_slope(h)
            off = (h % 2) * HD
            for m in range(2):
                S_.dma("sp", lambda e, m=m: e.dma_start(out=Qh[m * HD:(m + 1) * HD, hb, :], in_=QT[m * 8 + hp][off:off + HD, :]),
                       t_Qh[hb][m], writes=[t_Qh[hb][m]])
                S_.dma("sp", lambda e, m=m: e.dma_start(out=Kh[m * HD:(m + 1) * HD, hb, :], in_=KT[m * 8 + hp][off:off + HD, :]),
                       t_Kh[hb][m], writes=[t_Kh[hb][m]])
            S_.dma("sp", lambda e: e.dma_start(
                out=Vh[:, hb], in_=Vs[:, h * 128:(h + 1) * 128].rearrange("(kc p) e -> p kc e", p=128)), t_Vh[hb], writes=[t_Vh[hb]])
            S_.op("act", lambda e: e.activation(out=Dh[:, hb, :], in_=AB[:], func=AF.Exp, scale=-slope),
                  reads=[t_AB], writes=[t_Dh[hb]])

        steps = []
        blkno = 0
        for h in range(H):
            for qb in range(NB):
                kcs = [kc for kc in range(NKC) if needed(h, kc, qb)]
                for i, kc in enumerate(kcs):
                    steps.append((h, qb, kc, i == 0, i == len(kcs) - 1, blkno))
                blkno += 1

        def emit_av(p):
            slots, h, kc, first, last = p
            hb = h % 2

            def av(e):
                for m in range(2):
                    e.matmul(psO[m][0][:], lhsT=Vh[:, hb, kc, :], rhs=ET[:, slots[m], :], start=first, stop=last)
                    ins = e.matmul(psZ[m][0][:], lhsT=onesB[:], rhs=ET[:, slots[m], :], start=first, stop=last)
                return ins
            S_.op("pe", av, reads=[t_Vh[hb], t_ET[slots[0]], t_ET[slots[1]]],
                  writes=[psO[0][1], psO[1][1], psZ[0][1], psZ[1][1]])

        def epilogue_part1(h, qb, blk):
            pb_ = P.ring("epi", 2)
            for m in range(2):
                S_.op("act", lambda e, m=m: e.activation(out=rz[:, m, :], in_=psZ[m][0][:], func=AF.Ln),
                      reads=[psZ[m][1]], writes=[t_rz[m]])
                if m == 0:
                    S_.op("dve", lambda e: e.tensor_copy(out=ot[:, 0, :], in_=psO[0][0][:]), reads=[psO[0][1]], writes=[t_ot[0]])
                else:
                    S_.op("dve", lambda e: e.tensor_scalar(out=ot[:, 1, :], in0=psO[1][0][:], scalar1=cst[:, C_NLAM:C_NLAM + 1],
                                                           scalar2=None, op0=ALU.mult), reads=[psO[1][1]], writes=[t_ot[1]])
            for m in range(2):
                S_.op("act", lambda e, m=m: e.activation(out=rz[:, m, :], in_=rz[:, m, :], func=AF.Exp, scale=-1.0),
                      reads=[t_rz[m]], writes=[t_rz[m]])
            S_.op("dve", lambda e: e.tensor_tensor(out=ot[:, 0, :], in0=ot[:, 0, :], in1=rz[:, 0, :], op=ALU.mult),
                  reads=[t_ot[0], t_rz[0]], writes=[t_ot[0]])
            pe_ = "dve" if (pool_busy and blk < NBUSY) else "pool"
            S_.op(pe_, lambda e: e.tensor_tensor(out=ot[:, 1, :], in0=ot[:, 1, :], in1=rz[:, 1, :], op=ALU.mult),
                  reads=[t_ot[1], t_rz[1]], writes=[t_ot[1]])
            S_.op("dve", lambda e: e.tensor_tensor(out=oo[:, pb_, :], in0=ot[:, 0, :], in1=ot[:, 1, :], op=ALU.add),
                  reads=[t_ot[0], t_ot[1]], writes=[t_oo[pb_]])
            S_.op(pe_, lambda e: e.tensor_tensor(out=osq[:, pb_, :], in0=oo[:, pb_, :], in1=oo[:, pb_, :], op=ALU.mult),
                  reads=[t_oo[pb_]], writes=[t_osq[pb_]])
            deferred.append([3, lambda: epilogue_part2(h, qb, pb_)])

        def retire(p):
            emit_av(p)
            slots, h, kc, first, last = p
            if last:
                epilogue_part1(h, p_qb[id(p)], p_blk[id(p)])

        p_qb = {}
        p_blk = {}
        pend = []
        head_loads(0)
        for (h, qb, kc, first, last, blk) in steps:
            hb = h % 2
            pbuf = (h // 2) % 2
            off = (h % 2) * HD
            if first and qb == 0:
                hstep = 0
            hstep += 1
            if hstep == LAG + 2 and h + 1 < H:
                head_loads(h + 1)
            sl = []
            sbanks = [psS[P.ring("psS", 3)] for m in range(2)]

            def sc(e, kc=kc, qb=qb, sbanks=sbanks, hb=hb):
                for m in range(2):
                    ins = e.matmul(sbanks[m][0][:], lhsT=Kh[m * HD:(m + 1) * HD, hb, kc * 128:(kc + 1) * 128],
                                   rhs=Qh[m * HD:(m + 1) * HD, hb, qb * TB:(qb + 1) * TB], start=True, stop=True)
                return ins
            S_.op("pe", sc, reads=t_Kh[hb] + t_Qh[hb], writes=[sbanks[0][1], sbanks[1][1]])
            if len(pend) >= LAG:
                retire(pend.pop(0))
            if deferred:
                deferred[0][0] -= 1
                if deferred[0][0] <= 0:
                    deferred.pop(0)[1]()
            a0 = OFF + qb * TB - kc * 128
            for m in range(2):
                r = P.ring("eb", NE)
                s_ = P.ring("ET", NET)
                sl.append(s_)
                S_.op("act", lambda e, r=r, bk=sbanks[m][0]: e.activation(out=eb[:, r, :], in_=bk[:], func=AF.Exp, scale=SCALE),
                      reads=[sbanks[m][1]], writes=[t_eb[r]])
                eng = "pool" if (P.ring("mulsel", 5) in (1, 3) and not (pool_busy and blk < NBUSY)) else "dve"
                S_.op(eng, lambda e, r=r, s_=s_, a0=a0, hb=hb: e.tensor_tensor(out=ET[:, s_, :], in0=eb[:, r, :],
                                                                              in1=Dh[:, hb, a0:a0 + TB], op=ALU.mult),
                      reads=[t_eb[r], t_Dh[hb]], writes=[t_ET[s_]])
            item = (sl, h, kc, first, last)
            p_qb[id(item)] = qb
            p_blk[id(item)] = blk
            pend.append(item)
        while pend:
            retire(pend.pop(0))
        while deferred:
            deferred.pop(0)[1]()
        P.finish()

        P = Phase(nc, "pC")
        S_ = P.S
        xt, t_xt = P.sb("xt", [128, 4, D], F32, nT=4)
        x1T, t_x1 = P.sb("x1T", [128, NCH, TB], F32, nT=NCH)
        hT, t_hT = P.sb("hT", [128, NCH, TB], BF16, nT=NCH)
        aT, t_aT = P.sb("aT", [128, NCH, TB], BF16, nT=NCH)
        rtmp, t_rtmp = P.sb("rtmp", [128, 4, TB], F32, nT=4)
        sqb, t_sqb = P.sb("sqb", [128, 4, TB], BF16, nT=4)
        rs, t_rs = P.sb("rs", [128, TB], F32)
        tmpA, t_tmpA = P.sb("tmpA", [128, 2, 4, TB], F32, nT=2)
        sig, t_sig = P.sb("sig", [128, 2, TB], F32, nT=2)
        glu, t_glu = P.sb("glu", [128, 2, 4, TB], BF16, nT=2)
        pst = [P.ps("pst%d" % i) for i in range(2)]
        banks = [P.ps("pso%d" % i) for i in range(4)]
        psn, t_psn = P.ps("psn")
        t_scr = T("scrC")
        pw1_order = [0, 4, 1, 5, 2, 6, 3, 7]
        plan = []
        for tb in range(NB):
            plan += [wb_o[j] for j in range(4)] + mlp_plan(0) + [wb_pw1[j] for j in pw1_order]
        ws = WStream(P, plan)
        for tb in range(NB):
            for tc in range(4):
                r0 = tb * TB + tc * 128
                S_.dma("sp", lambda e, tc=tc, r0=r0: e.dma_start(out=xt[:, tc, :], in_=x[r0:r0 + 128, :]), t_xt[tc],
                       writes=[t_xt[tc]])
            S_.dma("sp", lambda e, tb=tb: e.dma_start(out=aT[:], in_=OT[:, tb * TB:(tb + 1) * TB].rearrange("(c p) t -> p c t", p=128)),
                   t_aT[0], writes=t_aT)
            ws.prefetch()
            for c in range(NCH):
                pb, t_pb = pst[c % 2]

                def trf(e, c=c, pb=pb):
                    for tc in range(4):
                        ins = e.transpose(pb[:, tc * 128:(tc + 1) * 128], xt[:, tc, c * 128:(c + 1) * 128], identF[:])
                    return ins
                S_.op("pe", trf, reads=t_xt, writes=[t_pb])
                if c % 2 == 0:
                    S_.op("act", lambda e, c=c, pb=pb: e.activation(out=x1T[:, c, :], in_=pb[:], func=AF.Copy), reads=[t_pb],
                          writes=[t_x1[c]])
                else:
                    S_.op("dve", lambda e, c=c, pb=pb: e.tensor_copy(out=x1T[:, c, :], in_=pb[:]), reads=[t_pb], writes=[t_x1[c]])

            def evac_o(ti, mc, bank, t_bank):
                dc = ti * 4 + mc
                S_.op("dve", lambda e: e.tensor_tensor(out=x1T[:, dc, :], in0=x1T[:, dc, :], in1=bank[:], op=ALU.add),
                      reads=[t_bank, t_x1[dc]], writes=[t_x1[dc]])
            dense(P, ws, aT, t_aT, 4, banks, evac_o)
            norm_fm(P, x1T, t_x1, CG_M0, hT, t_hT, psn, t_psn, rs, t_rs, sqb, t_sqb)
            mlp(P, ws, x1T, t_x1, hT, t_hT, aT, t_aT, banks, rtmp, t_rtmp)
            S_.dma("sp", lambda e, tb=tb: e.dma_start(out=X2T[:, tb * TB:(tb + 1) * TB].rearrange("(c p) t -> p c t", p=128), in_=x1T[:]),
                   t_x1[0], reads=t_x1, writes=[t_scr])
            norm_fm(P, x1T, t_x1, CG_C, hT, t_hT, psn, t_psn, rs, t_rs, sqb, t_sqb)

            def evac_pw1(ti, mc, bank, t_bank, tb=tb):
                j = pw1_order[ti]
                pb_ = (ti // 2) % 2
                if j < 4:
                    cc = j * 4 + mc
                    S_.op("act", lambda e: e.activation(out=tmpA[:, pb_, mc, :], in_=bank[:], func=AF.Identity, bias=col(CB_PW1, cc)),
                          reads=[t_bank], writes=[t_tmpA[pb_]])
                else:
                    cc = j * 4 + mc
                    r = P.ring("sig", 2)
                    S_.op("act", lambda e: e.activation(out=sig[:, r, :], in_=bank[:], func=AF.Sigmoid, bias=col(CB_PW1, cc)),
                          reads=[t_bank], writes=[t_sig[r]])
                    eng = "dve" if mc % 2 == 0 else "pool"
                    S_.op(eng, lambda e: e.tensor_tensor(out=glu[:, pb_, mc, :], in0=tmpA[:, pb_, mc, :], in1=sig[:, r, :], op=ALU.mult),
                          reads=[t_tmpA[pb_], t_sig[r]], writes=[t_glu[pb_]])
                    if mc == 3:
                        c0 = (j - 4) * 4 * 128
                        dst = GLU[c0:c0 + 512, tb * TB:(tb + 1) * TB].rearrange("(c p) t -> p c t", p=128)
                        S_.dma("sp", lambda e: e.dma_start(out=dst, in_=glu[:, pb_]), t_glu[pb_], reads=[t_glu[pb_]], writes=[t_scr])
            dense(P, ws, hT, t_hT, 8, banks, evac_pw1)
        P.finish()

        P = Phase(nc, "pD")
        S_ = P.S
        P.bgq = []
        P.bgn = 2
        P.bgn_up = 6
        P.bgn_dn = 2
        ystg, t_ystg = P.sb("ystg", [128, 2, D], F32, nT=2)
        x1T, t_x1 = P.sb("x1T", [128, NCH, TB], F32, nT=NCH)
        hT, t_hT = P.sb("hT", [128, NCH, TB], BF16, nT=NCH)
        aT, t_aT = P.sb("aT", [128, NCH, TB], BF16, nT=NCH)
        gl, t_gl = P.sb("gl", [128, NCH, TB + 2 * CPAD], BF16)
        vb, t_vb = P.sb("vb", [128, NCH, TB], BF16, nT=NCH)
        cacc, t_cacc = P.sb("cacc", [128, 2, TB], F32, nT=2)
        rtmp, t_rtmp = P.sb("rtmp", [128, 4, TB], F32, nT=4)
        sqb, t_sqb = P.sb("sqb", [128, 4, TB], BF16, nT=4)
        rs, t_rs = P.sb("rs", [128, TB], F32)
        mu, t_mu = P.sb("mu", [128, TB], F32)
        nmr, t_nmr = P.sb("nmr", [128, TB], F32)
        pst = [P.ps("pst%d" % i) for i in range(2)]
        banks = [P.ps("pso%d" % i) for i in range(4)]
        psn, t_psn = P.ps("psn")
        psm, t_psm = P.ps("psm")
        t_scr = T("scrD")
        plan = []
        for tb in range(NB):
            plan += [wb_pw2[j] for j in range(4)] + mlp_plan(1)
        ws = WStream(P, plan)

        def load_gl(tb):
            t0 = tb * TB
            lo = max(t0 - CPAD, 0)
            hi = min(t0 + TB + CPAD, S)
            if lo > t0 - CPAD:
                S_.op("pool", lambda e: e.memset(gl[:, :, 0:CPAD], 0.0), writes=[t_gl])
            if hi < t0 + TB + CPAD:
                S_.op("pool", lambda e: e.memset(gl[:, :, TB + CPAD:TB + 2 * CPAD], 0.0), writes=[t_gl])
            o0 = lo - (t0 - CPAD)
            S_.dma("sp", lambda e: e.dma_start(out=gl[:, :, o0:o0 + hi - lo],
                                               in_=GLU[:, lo:hi].rearrange("(c p) t -> p c t", p=128)), t_gl, writes=[t_gl])

        def conv_ops():
            ops = []
            for c0 in range(0, NCH, 2):
                for k in range(CW):
                    for c in (c0, c0 + 1):
                        r = c % 2

                        def f(c=c, k=k, r=r):
                            if k == 0:
                                S_.op("dve", lambda e: e.tensor_scalar(out=cacc[:, r, :], in0=gl[:, c, 0:TB], scalar1=wdw[:, c, 0:1],
                                                                       scalar2=col(CB_DW, c), op0=ALU.mult, op1=ALU.add),
                                      reads=[t_gl], writes=[t_cacc[r]])
                            elif k < CW - 1:
                                S_.op("dve", lambda e: e.scalar_tensor_tensor(out=cacc[:, r, :], in0=gl[:, c, k:k + TB],
                                                                              scalar=wdw[:, c, k:k + 1], in1=cacc[:, r, :],
                                                                              op0=ALU.mult, op1=ALU.add),
                                      reads=[t_gl, t_cacc[r]], writes=[t_cacc[r]])
                            else:
                                S_.op("dve", lambda e: e.scalar_tensor_tensor(out=vb[:, c, :], in0=gl[:, c, k:k + TB],
                                                                              scalar=wdw[:, c, k:k + 1], in1=cacc[:, r, :],
                                                                              op0=ALU.mult, op1=ALU.add),
                                      reads=[t_gl, t_cacc[r]], writes=[t_vb[c]])
                        ops.append(f)
            return ops

        load_gl(0)
        ctmp, t_ctmp = P.sb("ctmp", [128, 2, TB], F32, nT=2)
        cacp, t_cacp = P.sb("cacp", [128, 2, TB], F32, nT=2)
        NDV = 16

        def conv_ops_pool(c0):
            ops = []
            for k in range(CW):
                for c in (c0, c0 + 1):
                    r = c % 2

                    def f(c=c, k=k, r=r):
                        if k == 0:
                            S_.op("act", lambda e: e.activation(out=cacp[:, r, :], in_=gl[:, c, 0:TB], func=AF.Identity,
                                                                scale=wdw[:, c, 0:1], bias=col(CB_DW, c)),
                                  reads=[t_gl], writes=[t_cacp[r]])
                            return
                        S_.op("act", lambda e: e.activation(out=ctmp[:, r, :], in_=gl[:, c, k:k + TB], func=AF.Copy,
                                                            scale=wdw[:, c, k:k + 1]), reads=[t_gl], writes=[t_ctmp[r]])
                        if k < CW - 1:
                            S_.op("pool", lambda e: e.tensor_tensor(out=cacp[:, r, :], in0=cacp[:, r, :], in1=ctmp[:, r, :], op=ALU.add),
                                  reads=[t_cacp[r], t_ctmp[r]], writes=[t_cacp[r]])
                        else:
                            S_.op("pool", lambda e: e.tensor_tensor(out=vb[:, c, :], in0=cacp[:, r, :], in1=ctmp[:, r, :], op=ALU.add),
                                  reads=[t_cacp[r], t_ctmp[r]], writes=[t_vb[c]])
                    ops.append(f)
            return ops

        dv_ops = [f for f in conv_ops()][:NDV * CW]
        pl_ops = []
        for c0 in range(NDV, NCH, 2):
            pl_ops += conv_ops_pool(c0)
        while dv_ops or pl_ops:
            if dv_ops:
                dv_ops.pop(0)()
            if pl_ops:
                pl_ops.pop(0)()
        for tb in range(NB):
            S_.dma("sp", lambda e, tb=tb: e.dma_start(out=x1T[:], in_=X2T[:, tb * TB:(tb + 1) * TB].rearrange("(c p) t -> p c t", p=128)),
                   t_x1[0], writes=t_x1)
            ws.prefetch()
            for c in range(NCH):
                S_.op("pe", lambda e, c=c: e.matmul(psm[:], lhsT=onesB[:], rhs=vb[:, c, :], start=(c == 0), stop=(c == NCH - 1)),
                      reads=[t_vb[c]], writes=[t_psm])
                r2 = P.ring("sqb", 4)
                eng = "pool" if c % 2 == 0 else "act"
                if eng == "pool":
                    S_.op("pool", lambda e, c=c, r2=r2: e.tensor_tensor(out=sqb[:, r2, :], in0=vb[:, c, :], in1=vb[:, c, :], op=ALU.mult),
                          reads=[t_vb[c]], writes=[t_sqb[r2]])
                else:
                    S_.op("act", lambda e, c=c, r2=r2: e.activation(out=sqb[:, r2, :], in_=vb[:, c, :], func=AF.Square),
                          reads=[t_vb[c]], writes=[t_sqb[r2]])
                S_.op("pe", lambda e, c=c, r2=r2: e.matmul(psn[:], lhsT=onesB[:], rhs=sqb[:, r2, :], start=(c == 0), stop=(c == NCH - 1)),
                      reads=[t_sqb[r2]], writes=[t_psn])
            S_.op("dve", lambda e: e.tensor_scalar(out=mu[:], in0=psm[:], scalar1=1.0 / D, scalar2=None, op0=ALU.mult),
                  reads=[t_psm], writes=[t_mu])
            S_.op("dve", lambda e: e.tensor_tensor(out=nmr[:], in0=mu[:], in1=mu[:], op=ALU.mult), reads=[t_mu], writes=[t_nmr])
            S_.op("dve", lambda e: e.scalar_tensor_tensor(out=rs[:], in0=psn[:], scalar=1.0 / D, in1=nmr[:], op0=ALU.mult,
                                                          op1=ALU.subtract), reads=[t_psn, t_nmr], writes=[t_rs])
            S_.op("dve", lambda e: e.tensor_scalar(out=rs[:], in0=rs[:], scalar1=LN_EPS, scalar2=None, op0=ALU.add),
                  reads=[t_rs], writes=[t_rs])
            S_.op("act", lambda e: e.activation(out=rs[:], in_=rs[:], func=AF.Ln), reads=[t_rs], writes=[t_rs])
            S_.op("act", lambda e: e.activation(out=rs[:], in_=rs[:], func=AF.Exp, scale=-0.5), reads=[t_rs], writes=[t_rs])
            S_.op("dve", lambda e: e.scalar_tensor_tensor(out=nmr[:], in0=mu[:], scalar=-1.0, in1=rs[:], op0=ALU.mult, op1=ALU.mult),
                  reads=[t_mu, t_rs], writes=[t_nmr])
            for c in range(NCH):
                eng = "dve" if c % 2 == 0 else "pool"
                r = P.ring("rtmp", 4)
                S_.op(eng, lambda e, c=c, r=r: e.tensor_tensor(out=rtmp[:, r, :], in0=vb[:, c, :], in1=rs[:], op=ALU.mult),
                      reads=[t_rs, t_vb[c]], writes=[t_rtmp[r]])
                S_.op(eng, lambda e, c=c, r=r: e.tensor_tensor(out=rtmp[:, r, :], in0=rtmp[:, r, :], in1=nmr[:], op=ALU.add),
                      reads=[t_nmr, t_rtmp[r]], writes=[t_rtmp[r]])
                S_.op("act", lambda e, c=c, r=r: e.activation(out=aT[:, c, :], in_=rtmp[:, r, :], func=AF.Silu, scale=col(CLN_G, c),
                                                             bias=col(CLN_B, c)), reads=[t_rtmp[r]], writes=[t_aT[c]])
            if tb + 1 < NB:
                load_gl(tb + 1)
                P.bgq = conv_ops()

            def evac_pw2(ti, mc, bank, t_bank):
                dc = ti * 4 + mc
                S_.op("dve", lambda e: e.scalar_tensor_tensor(out=x1T[:, dc, :], in0=bank[:], scalar=col(CB_PW2, dc), in1=x1T[:, dc, :],
                                                             op0=ALU.add, op1=ALU.add), reads=[t_bank, t_x1[dc]], writes=[t_x1[dc]])
            dense(P, ws, aT, t_aT, 4, banks, evac_pw2)
            norm_fm(P, x1T, t_x1, CG_M1, hT, t_hT, psn, t_psn, rs, t_rs, sqb, t_sqb)
            mlp(P, ws, x1T, t_x1, hT, t_hT, aT, t_aT, banks, rtmp, t_rtmp)
            while P.bgq:
                P.bgq.pop(0)()
            norm_fm(P, x1T, t_x1, CG_F, x1T, t_x1, psn, t_psn, rs, t_rs, sqb, t_sqb)
            for tc in range(4):
                yb = P.ring("ystg", 2)
                for dq in range(4):
                    pb, t_pb = pst[(tc * 4 + dq) % 2]

                    def trb(e, tc=tc, dq=dq, pb=pb):
                        for i in range(4):
                            c = dq * 4 + i
                            ins = e.transpose(pb[:, i * 128:(i + 1) * 128], x1T[:, c, tc * 128:(tc + 1) * 128], identF[:])
                        return ins
                    S_.op("pe", trb, reads=[t_x1[dq * 4 + i] for i in range(4)], writes=[t_pb])
                    if dq % 2 == 0:
                        S_.op("act", lambda e, yb=yb, dq=dq, pb=pb: e.activation(out=ystg[:, yb, dq * 512:(dq + 1) * 512], in_=pb[:], func=AF.Copy),
                              reads=[t_pb], writes=[t_ystg[yb]])
                    else:
                        S_.op("dve", lambda e, yb=yb, dq=dq, pb=pb: e.tensor_copy(out=ystg[:, yb, dq * 512:(dq + 1) * 512], in_=pb[:]),
                              reads=[t_pb], writes=[t_ystg[yb]])
                r0 = tb * TB + tc * 128
                S_.dma("sp", lambda e, yb=yb, r0=r0: e.dma_start(out=y[r0:r0 + 128, :], in_=ystg[:, yb, :]), t_ystg[yb],
                       reads=[t_ystg[yb]], writes=[t_scr])
        P.finish()
    return nc


_NC_CACHE = {}

WNAMES = ["attn_norm_g", "w_qkv", "lam_q1", "lam_k1", "lam_q2", "lam_k2", "subln_g", "w_o", "conv_norm_g",
          "conv_w_pw1", "conv_b_pw1", "conv_w_dw", "conv_b_dw", "conv_ln_g", "conv_ln_b", "conv_w_pw2",
          "conv_b_pw2", "mlp_norm_g", "w_up", "w_down", "final_norm_g"]


def prep_weights(inputs):
    f = lambda a: np.ascontiguousarray(np.asarray(a, dtype=np.float32))
    w = {}
    w["attn_norm_g"] = f(inputs["attn_norm_g"]).reshape(D)
    w["w_qkv"] = f(inputs["w_qkv"]).reshape(D, 3 * D)
    for n in ("lam_q1", "lam_k1", "lam_q2", "lam_k2"):
        w[n] = f(inputs[n]).reshape(1, HD)
    w["subln_g"] = f(inputs["subln_g"]).reshape(2 * HD)
    w["w_o"] = f(inputs["w_o"]).reshape(D, D)
    w["conv_norm_g"] = f(inputs["conv_norm_g"]).reshape(D)
    w["conv_w_pw1"] = f(inputs["conv_w_pw1"]).reshape(D, 2 * D)
    w["conv_b_pw1"] = f(inputs["conv_b_pw1"]).reshape(2 * D)
    w["conv_w_dw"] = f(inputs["conv_w_dw"]).reshape(CW, D)
    w["conv_b_dw"] = f(inputs["conv_b_dw"]).reshape(D)
    w["conv_ln_g"] = f(inputs["conv_ln_g"]).reshape(D)
    w["conv_ln_b"] = f(inputs["conv_ln_b"]).reshape(D)
    w["conv_w_pw2"] = f(inputs["conv_w_pw2"]).reshape(D, D)
    w["conv_b_pw2"] = f(inputs["conv_b_pw2"]).reshape(D)
    w["mlp_norm_g"] = f(inputs["mlp_norm_g"]).reshape(2, D)
    w["w_up"] = f(inputs["w_up"]).reshape(2, D, DFF)
    w["w_down"] = f(inputs["w_down"]).reshape(2, DFF, D)
    w["final_norm_g"] = f(inputs["final_norm_g"]).reshape(D)
    return w


def kernel(**inputs):
    xp = np.asarray(inputs["x_prompt"], dtype=np.float32)
    xs = np.asarray(inputs["x_sample"], dtype=np.float32)
    S = xp.shape[1]
    seqs = [xp[i] for i in range(xp.shape[0])] + [xs[i] for i in range(xs.shape[0])]
    n = len(seqs)
    if S not in _NC_CACHE:
        _NC_CACHE[S] = build(S)
    nc = _NC_CACHE[S]
    w = prep_weights(inputs)
    in_maps = []
    for i in range(n):
        m = dict(w)
        m["x"] = np.ascontiguousarray(seqs[i])
        in_maps.append(m)
    res = run_bass_kernel_spmd(nc, in_maps, core_ids=list(range(n)))
    outs = [np.asarray(r["y"], dtype=np.float32) for r in res.results]
    yp = np.stack(outs[:xp.shape[0]], axis=0)
    ys = np.stack(outs[xp.shape[0]:], axis=0)
    return (yp, ys)
```
